# Optimizing a Trainium2 kernel written in Bass

```python
import jax, jax.numpy as jnp
from jax import lax
import numpy as np

D_MODEL = 2048
BATCH = 4
SEQ = 2048
DEPTH = 1
DEC_BATCH = 128
DEC_SEQ = 1
PAST_LEN = 16384
PAGE_SIZE = 128

D_POOL = D_MODEL // 2
POOL_WINDOWS = (2, 4, 8, 16)
N_POOL_GROUPS = len(POOL_WINDOWS)
POOL_GROUP = D_POOL // N_POOL_GROUPS
D_SSM = D_MODEL // 2
SSM_GROUP = 16
N_SSM_GROUPS = D_SSM // SSM_GROUP
SSM_STATE = 64
DT_MIN = 1e-3
DT_MAX = 1e-1
D_FF = 3 * D_MODEL
CONV_W = 3
D_PLE = 256
D_IN = D_POOL + D_SSM + 2 * D_MODEL
EPS = 1e-6

kernel_name = 'pool_s5_gated_hybrid_step'


def _rmsnorm(x, g):
    xf = x.astype(jnp.float32)
    y = xf * lax.rsqrt(jnp.mean(xf * xf, axis=-1, keepdims=True) + EPS)
    return (y * g.astype(jnp.float32)).astype(x.dtype)


def _causal_pool(u, past, past_valid, w_pool, pool_scale):
    n, t, _ = u.shape
    lb = past.shape[1]
    ext = jnp.concatenate([past.astype(u.dtype), u], axis=1)
    extf = ext.astype(jnp.float32)
    valid = jnp.concatenate([jnp.full((lb,), past_valid, jnp.float32),
                             jnp.ones((t,), jnp.float32)])
    cc = jnp.concatenate([jnp.zeros((1,), jnp.float32), jnp.cumsum(valid)])
    hi = lb + 1
    outs = []
    for k, w in enumerate(POOL_WINDOWS):
        seg = extf[:, :, k * POOL_GROUP:(k + 1) * POOL_GROUP]
        cs = jnp.pad(jnp.cumsum(seg, axis=1), ((0, 0), (1, 0), (0, 0)))
        lo = lb + 1 - w
        win_sum = cs[:, hi:hi + t] - cs[:, lo:lo + t]
        count = (cc[hi:hi + t] - cc[lo:lo + t])[None, :, None]
        diff = win_sum / count - seg[:, lb:]
        outs.append(jnp.einsum('ntc,cd->ntd', diff.astype(u.dtype), w_pool[k]))
    y = jnp.concatenate(outs, axis=-1) * pool_scale
    return y, ext[:, -lb:]


def _causal_dwconv(a, past, w_conv, b_conv):
    t = a.shape[1]
    lb = past.shape[1]
    ext = jnp.concatenate([past.astype(a.dtype), a], axis=1)
    y = b_conv
    for j in range(CONV_W):
        y = y + w_conv[j] * ext[:, j:j + t]
    return y, ext[:, -lb:]


def _cmul_combine(e1, e2):
    a1r, a1i, b1r, b1i = e1
    a2r, a2i, b2r, b2i = e2
    ar = a2r * a1r - a2i * a1i
    ai = a2r * a1i + a2i * a1r
    br = a2r * b1r - a2i * b1i + b2r
    bi = a2r * b1i + a2i * b1r + b2i
    return (ar, ai, br, bi)


def _s5(u, h0_re, h0_im, lam_re, lam_im, log_dt, b_re, b_im, c_re, c_im, d_skip):
    n, t, _ = u.shape
    uf = u.astype(jnp.float32).reshape(n, t, N_SSM_GROUPS, SSM_GROUP)
    dt = jnp.exp(log_dt.astype(jnp.float32))[:, None]
    lr = lam_re.astype(jnp.float32)
    li = lam_im.astype(jnp.float32)
    mag = jnp.exp(lr * dt)
    abar_re = mag * jnp.cos(li * dt)
    abar_im = mag * jnp.sin(li * dt)
    nr = abar_re - 1.0
    ni = abar_im
    den = lr * lr + li * li
    coef_re = ((nr * lr + ni * li) / den)[..., None]
    coef_im = ((ni * lr - nr * li) / den)[..., None]
    br = b_re.astype(jnp.float32)
    bi = b_im.astype(jnp.float32)
    bbar_re = coef_re * br - coef_im * bi
    bbar_im = coef_re * bi + coef_im * br
    bu_re = jnp.einsum('ntgh,gph->ntgp', uf, bbar_re)
    bu_im = jnp.einsum('ntgh,gph->ntgp', uf, bbar_im)
    a_re = jnp.broadcast_to(abar_re, bu_re.shape)
    a_im = jnp.broadcast_to(abar_im, bu_re.shape)
    acr, aci, hzr, hzi = lax.associative_scan(_cmul_combine, (a_re, a_im, bu_re, bu_im), axis=1)
    h0r = h0_re.astype(jnp.float32)[:, None]
    h0i = h0_im.astype(jnp.float32)[:, None]
    h_re = acr * h0r - aci * h0i + hzr
    h_im = acr * h0i + aci * h0r + hzi
    y = (jnp.einsum('ntgp,ghp->ntgh', h_re, c_re.astype(jnp.float32))
         - jnp.einsum('ntgp,ghp->ntgh', h_im, c_im.astype(jnp.float32))
         + d_skip.astype(jnp.float32).reshape(N_SSM_GROUPS, SSM_GROUP) * uf)
    return y.reshape(n, t, D_SSM).astype(u.dtype), h_re[:, -1], h_im[:, -1]


def _layer(x, p, pool_past, pool_valid, h0_re, h0_im, conv_past,
           g_mix, w_in, w_pool, pool_scale, lam_re, lam_im, log_dt, b_re, b_im,
           c_re, c_im, d_skip, w_glu, w_branch_pool, w_branch_ssm, w_out,
           g_ffn, w_up, w_conv, b_conv, w_down, g_ple, w_ple_gate, w_ple):
    h = _rmsnorm(x, g_mix)
    z = h @ w_in
    u_pool, u_ssm, gate_pool, gate_ssm = jnp.split(
        z, [D_POOL, D_POOL + D_SSM, D_POOL + D_SSM + D_MODEL], axis=-1)
    y_pool, pool_new = _causal_pool(u_pool, pool_past, pool_valid, w_pool, pool_scale)
    y_ssm, h_re, h_im = _s5(u_ssm, h0_re, h0_im, lam_re, lam_im, log_dt,
                            b_re, b_im, c_re, c_im, d_skip)
    g = jax.nn.gelu(y_ssm) @ w_glu
    y_ssm = g[..., :D_SSM] * jax.nn.sigmoid(g[..., D_SSM:])
    merged = (jax.nn.sigmoid(gate_pool) * (y_pool @ w_branch_pool)
              + jax.nn.sigmoid(gate_ssm) * (y_ssm @ w_branch_ssm))
    x = x + merged @ w_out
    h = _rmsnorm(x, g_ffn)
    up = h @ w_up
    a, v = jnp.split(up, [D_FF], axis=-1)
    a, conv_new = _causal_dwconv(a, conv_past, w_conv, b_conv)
    x = x + (jax.nn.gelu(a) * v) @ w_down
    gate = jax.nn.sigmoid(_rmsnorm(x, g_ple) @ w_ple_gate)
    x = x + gate * (p.astype(x.dtype) @ w_ple)
    return x, pool_new, h_re.astype(x.dtype), h_im.astype(x.dtype), conv_new


def setup_inputs(seed: int = 0) -> dict:
    key = jax.random.key(seed)
    ks = jax.random.split(key, 40)
    f32 = jnp.float32

    def nrm(k, shape, scale):
        return jax.random.normal(k, shape, f32) * scale

    pool_buf = min(max(POOL_WINDOWS) - 1, PAST_LEN)
    conv_buf = min(CONV_W - 1, PAST_LEN)
    n_idx = jnp.arange(SSM_STATE, dtype=f32)
    lam_re = -0.5 + nrm(ks[12], (DEPTH, N_SSM_GROUPS, SSM_STATE), 0.01)
    lam_im = jnp.pi * n_idx + nrm(ks[13], (DEPTH, N_SSM_GROUPS, SSM_STATE), 0.01)
    log_dt = jax.random.uniform(ks[14], (DEPTH, N_SSM_GROUPS), f32,
                                float(np.log(DT_MIN)), float(np.log(DT_MAX)))
    return {
        'x_prompt': nrm(ks[0], (BATCH, SEQ, D_MODEL), 1.0),
        'x_sample': nrm(ks[1], (DEC_BATCH, DEC_SEQ, D_MODEL), 1.0),
        'cache_pool': nrm(ks[2], (DEPTH, DEC_BATCH, pool_buf, D_POOL), 1.0),
        'state_ssm_re': nrm(ks[3], (DEPTH, DEC_BATCH, N_SSM_GROUPS, SSM_STATE), 0.1),
        'state_ssm_im': nrm(ks[4], (DEPTH, DEC_BATCH, N_SSM_GROUPS, SSM_STATE), 0.1),
        'cache_conv': nrm(ks[5], (DEPTH, DEC_BATCH, conv_buf, D_FF), 1.0),
        'p_prompt': nrm(ks[6], (DEPTH, BATCH, SEQ, D_PLE), 1.0),
        'p_sample': nrm(ks[7], (DEPTH, DEC_BATCH, DEC_SEQ, D_PLE), 1.0),
        'g_mix': 1.0 + nrm(ks[8], (DEPTH, D_MODEL), 0.02),
        'w_in': nrm(ks[9], (DEPTH, D_MODEL, D_IN), D_MODEL ** -0.5),
        'w_pool': nrm(ks[10], (DEPTH, N_POOL_GROUPS, POOL_GROUP, POOL_GROUP), POOL_GROUP ** -0.5),
        'pool_scale': 1.0 + nrm(ks[11], (DEPTH, D_POOL), 0.02),
        'ssm_lam_re': lam_re,
        'ssm_lam_im': lam_im,
        'ssm_log_dt': log_dt,
        'ssm_b_re': nrm(ks[15], (DEPTH, N_SSM_GROUPS, SSM_STATE, SSM_GROUP), (2 * SSM_GROUP) ** -0.5),
        'ssm_b_im': nrm(ks[16], (DEPTH, N_SSM_GROUPS, SSM_STATE, SSM_GROUP), (2 * SSM_GROUP) ** -0.5),
        'ssm_c_re': nrm(ks[17], (DEPTH, N_SSM_GROUPS, SSM_GROUP, SSM_STATE), SSM_STATE ** -0.5),
        'ssm_c_im': nrm(ks[18], (DEPTH, N_SSM_GROUPS, SSM_GROUP, SSM_STATE), SSM_STATE ** -0.5),
        'ssm_d': nrm(ks[19], (DEPTH, D_SSM), 1.0),
        'w_glu': nrm(ks[20], (DEPTH, D_SSM, 2 * D_SSM), D_SSM ** -0.5),
        'w_branch_pool': nrm(ks[21], (DEPTH, D_POOL, D_MODEL), D_POOL ** -0.5),
        'w_branch_ssm': nrm(ks[22], (DEPTH, D_SSM, D_MODEL), D_SSM ** -0.5),
        'w_out': nrm(ks[23], (DEPTH, D_MODEL, D_MODEL), D_MODEL ** -0.5),
        'g_ffn': 1.0 + nrm(ks[24], (DEPTH, D_MODEL), 0.02),
        'w_up': nrm(ks[25], (DEPTH, D_MODEL, 2 * D_FF), D_MODEL ** -0.5),
        'w_conv': nrm(ks[26], (DEPTH, CONV_W, D_FF), CONV_W ** -0.5),
        'b_conv': nrm(ks[27], (DEPTH, D_FF), 0.01),
        'w_down': nrm(ks[28], (DEPTH, D_FF, D_MODEL), D_FF ** -0.5),
        'g_ple': 1.0 + nrm(ks[29], (DEPTH, D_MODEL), 0.02),
        'w_ple_gate': nrm(ks[30], (DEPTH, D_MODEL, D_MODEL), D_MODEL ** -0.5),
        'w_ple': nrm(ks[31], (DEPTH, D_PLE, D_MODEL), D_PLE ** -0.5),
        'g_final': 1.0 + nrm(ks[32], (D_MODEL,), 0.02),
    }


def reference(x_prompt, x_sample, cache_pool, state_ssm_re, state_ssm_im, cache_conv,
              p_prompt, p_sample, g_mix, w_in, w_pool, pool_scale, ssm_lam_re, ssm_lam_im,
              ssm_log_dt, ssm_b_re, ssm_b_im, ssm_c_re, ssm_c_im, ssm_d, w_glu,
              w_branch_pool, w_branch_ssm, w_out, g_ffn, w_up, w_conv, b_conv, w_down,
              g_ple, w_ple_gate, w_ple, g_final):
    xp, xs = x_prompt, x_sample
    nb = x_prompt.shape[0]
    pool_buf = cache_pool.shape[2]
    conv_buf = cache_conv.shape[2]
    pool_p, re_p, im_p, conv_p = [], [], [], []
    pool_s, re_s, im_s, conv_s = [], [], [], []
    for i in range(DEPTH):
        lw = (g_mix[i], w_in[i], w_pool[i], pool_scale[i], ssm_lam_re[i], ssm_lam_im[i],
              ssm_log_dt[i], ssm_b_re[i], ssm_b_im[i], ssm_c_re[i], ssm_c_im[i], ssm_d[i],
              w_glu[i], w_branch_pool[i], w_branch_ssm[i], w_out[i], g_ffn[i], w_up[i],
              w_conv[i], b_conv[i], w_down[i], g_ple[i], w_ple_gate[i], w_ple[i])
        xp, a0, a1, a2, a3 = _layer(
            xp, p_prompt[i],
            jnp.zeros((nb, pool_buf, D_POOL), xp.dtype), 0.0,
            jnp.zeros((nb, N_SSM_GROUPS, SSM_STATE), jnp.float32),
            jnp.zeros((nb, N_SSM_GROUPS, SSM_STATE), jnp.float32),
            jnp.zeros((nb, conv_buf, D_FF), xp.dtype), *lw)
        pool_p.append(a0); re_p.append(a1); im_p.append(a2); conv_p.append(a3)
        xs, b0, b1, b2, b3 = _layer(
            xs, p_sample[i], cache_pool[i], 1.0, state_ssm_re[i], state_ssm_im[i],
            cache_conv[i], *lw)
        pool_s.append(b0); re_s.append(b1); im_s.append(b2); conv_s.append(b3)
    y_prompt = _rmsnorm(xp, g_final)
    y_sample = _rmsnorm(xs, g_final)
    return (y_prompt, y_sample,
            jnp.stack(pool_p, axis=0), jnp.stack(re_p, axis=0), jnp.stack(im_p, axis=0),
            jnp.stack(conv_p, axis=0),
            jnp.stack(pool_s, axis=0), jnp.stack(re_s, axis=0), jnp.stack(im_s, axis=0),
            jnp.stack(conv_s, axis=0))
```

```python
import numpy as np
import concourse.bass as bass
import concourse.mybir as mybir
from concourse.bass_utils import run_bass_kernel_spmd

F32 = mybir.dt.float32
F32R = mybir.dt.float32r
BF16 = mybir.dt.bfloat16
ALU = mybir.AluOpType
AF = mybir.ActivationFunctionType


class Op:
    __slots__ = ("eng", "emit", "deps", "idx", "signal", "semval", "dma", "semkey", "name")

    def __init__(self, eng, emit, dma, semkey, name):
        self.eng = eng
        self.emit = emit
        self.deps = []
        self.idx = -1
        self.signal = False
        self.semval = 0
        self.dma = dma
        self.semkey = semkey
        self.name = name


class Prog:
    ENGS = ("pe", "act", "dve", "pool", "sp")

    def __init__(self, nc):
        self.nc = nc
        self.q = {e: [] for e in self.ENGS}
        self.writers = {}
        self.readers = {}
        self.dma_keys = {}
        self.bulk_keys = set()

    def op(self, eng, emit, reads=(), writes=(), dma=False, semkey=None, name=""):
        o = Op(eng, emit, dma, semkey, name)
        deps = []
        for k in reads:
            deps += self.writers.get(k, [])
        for k in writes:
            for w in self.writers.get(k, ()):
                if not (dma and w.dma):
                    deps.append(w)
            deps += self.readers.get(k, [])
        seen = set()
        for d in deps:
            if id(d) in seen or d is o:
                continue
            if dma and d.dma and d.semkey == semkey and semkey in self.bulk_keys:
                continue
            seen.add(id(d))
            o.deps.append(d)
        for k in writes:
            if self.readers.get(k):
                self.writers[k] = [o]
                self.readers[k] = []
            else:
                self.writers.setdefault(k, []).append(o)
                if len(self.writers[k]) > 64:
                    self.writers[k] = self.writers[k][-64:]
        for k in reads:
            if k not in writes:
                self.readers.setdefault(k, []).append(o)
        o.idx = len(self.q[eng])
        self.q[eng].append(o)
        if dma:
            assert semkey is not None
            self.dma_keys.setdefault(semkey, []).append(o)
        return o

    def _needs_sync(self, c, p):
        if p.dma:
            return True
        if p.eng != c.eng:
            return True
        if c.dma:
            return True
        if p.eng == "pe":
            return False
        return True

    def emit(self):
        nc = self.nc
        for e in self.ENGS:
            for c in self.q[e]:
                for p in c.deps:
                    if (not p.dma) and self._needs_sync(c, p):
                        p.signal = True
            last = [o for o in self.q[e] if not o.dma]
            if last:
                last[-1].signal = True
        import contextlib

        stack = contextlib.ExitStack()
        sems = {}
        for e in ("pe", "act", "dve", "pool"):
            sems[e] = stack.enter_context(nc.semaphore("sem_" + e))
        dsems = {}
        for k in self.dma_keys:
            dsems[k] = stack.enter_context(nc.semaphore("dsem_%s" % (str(k).replace(" ", ""))))
        for e in ("pe", "act", "dve", "pool", "sp"):
            cnt = 0
            for o in self.q[e]:
                if o.dma:
                    continue
                if o.signal:
                    cnt += 1
                    o.semval = cnt
        for k, ops in self.dma_keys.items():
            cnt = 0
            for o in ops:
                cnt += 16
                o.semval = cnt
            if k in self.bulk_keys:
                for o in ops:
                    o.semval = cnt
        self.n_signals = {e: sum(1 for o in self.q[e] if o.signal) for e in self.ENGS}

        def run_queue(ename, eng):
            waited = {}
            for o in self.q[ename]:
                need = {}
                for p in o.deps:
                    if p.dma:
                        s = dsems[p.semkey]
                        v = p.semval
                    else:
                        if not p.signal:
                            continue
                        if not self._needs_sync(o, p):
                            continue
                        s = sems[p.eng]
                        v = p.semval
                    key = id(s)
                    if need.get(key, (None, 0))[1] < v:
                        need[key] = (s, v)
                for key, (s, v) in need.items():
                    if waited.get(key, 0) >= v:
                        continue
                    eng.wait_ge(s, v)
                    waited[key] = v
                inst = o.emit(eng)
                if o.dma:
                    inst.then_inc(dsems[o.semkey], 16)
                elif o.signal:
                    inst.then_inc(sems[o.eng], 1)
            return waited

        final_waits = []
        with nc.Block() as block:

            @block.sync
            def _(e):
                w = run_queue("sp", e)
                for k, ops in self.dma_keys.items():
                    e.wait_ge(dsems[k], 16 * len(ops))
                for en in ("pe", "act", "dve", "pool"):
                    if self.n_signals[en]:
                        e.wait_ge(sems[en], self.n_signals[en])

            @block.scalar
            def _(e):
                run_queue("act", e)

            @block.vector
            def _(e):
                run_queue("dve", e)

            @block.gpsimd
            def _(e):
                run_queue("pool", e)

            @block.tensor
            def _(e):
                run_queue("pe", e)

        stack.close()


DTSIZE = {F32: 4, F32R: 4, BF16: 2, mybir.dt.int32: 4}
I32 = mybir.dt.int32


class Region:
    def __init__(self, lo, hi):
        self.lo, self.hi, self.cur = lo, hi, lo

    def take(self, nbytes):
        nbytes = (nbytes + 31) // 32 * 32
        self.cur = (self.cur + 31) // 32 * 32
        off = self.cur
        self.cur += nbytes
        assert self.cur <= self.hi, ("region overflow", self.lo, self.hi, self.cur)
        return off


class KB:
    def __init__(self):
        self.nc = bass.Bass("TRN2", target_bir_lowering=False, dynamic_dma_scratch_size=256)
        self.nc.dge_precook = False
        self.P = Prog(self.nc)
        self.ps = [self.nc.alloc_psum_tensor("ps%d" % i, [128, 512], F32).ap() for i in range(8)]
        self.allocs = {}
        self.kcache = {}
        self.uid = 0
        self.dbg_outs = []

    def alloc(self, region, name, shape, dt=F32):
        n = 1
        for d in shape[1:]:
            n *= d
        nbytes = n * DTSIZE[dt]
        off = region.take(nbytes)
        self.uid += 1
        t = self.nc.alloc_sbuf_tensor_at("%s_%d" % (name, self.uid), list(shape), dt, offset=off)
        ap = t.ap()
        self.allocs[ap.name] = (off, off + (nbytes + 31) // 32 * 32)
        return ap

    KEYB = 32

    def keys(self, ap):
        nm = ap.name
        r = self.kcache.get(nm)
        if r is not None:
            return r
        if nm not in self.allocs:
            r = [nm]
        else:
            lo, hi = self.allocs[nm]
            r = [("b", k) for k in range(lo // self.KEYB, (hi - 1) // self.KEYB + 1)]
        self.kcache[nm] = r
        return r

    def _op(self, eng, fn, outs, ins, **kw):
        rd, wr = [], []
        for a in ins:
            rd += self.keys(a)
            if a.name.startswith("ps"):
                wr += self.keys(a)
        for a in outs:
            wr += self.keys(a)
        return self.P.op(eng, fn, reads=rd, writes=wr, **kw)

    def mm(self, out, lhsT, rhs, start=True, stop=True, tp=None):
        self._op("pe", lambda e: e.matmul(out, lhsT=lhsT, rhs=rhs, start=start, stop=stop, tile_position=tp),
                 [out], [lhsT, rhs])

    def tr(self, out, in_, ident):
        self._op("pe", lambda e: e.transpose(out, in_, ident), [out], [in_, ident])

    def act(self, out, in_, func, scale=1.0, bias=0.0):
        ins = [in_]
        if not isinstance(scale, (int, float)):
            ins.append(scale)
        if not isinstance(bias, (int, float)):
            ins.append(bias)
        self._op("act", lambda e: e.activation(out, in_, func, bias=bias, scale=scale), [out], ins)

    def _e(self, eng):
        return eng

    def tt(self, eng, out, a, b, op):
        self._op(eng, lambda e: e.tensor_tensor(out, a, b, op), [out], [a, b])

    def ts(self, eng, out, a, s1, op0, s2=None, op1=None):
        ins = [a] + [s for s in (s1, s2) if s is not None and not isinstance(s, (int, float))]
        if op1 is None:
            self._op(eng, lambda e: e.tensor_scalar(out, a, s1, None, op0), [out], ins)
        else:
            self._op(eng, lambda e: e.tensor_scalar(out, a, s1, s2, op0, op1), [out], ins)

    def stt(self, out, in0, scalar, in1, op0, op1):
        ins = [in0, in1] + ([] if isinstance(scalar, (int, float)) else [scalar])
        self._op("dve", lambda e: e.scalar_tensor_tensor(out, in0, scalar, in1, op0, op1), [out], ins)

    def cp(self, eng, out, in_):
        if eng == "act":
            self._op("act", lambda e: e.copy(out, in_), [out], [in_])
        else:
            self._op(eng, lambda e: e.tensor_copy(out, in_), [out], [in_])

    def ms(self, eng, ap, val):
        self._op(eng, lambda e: e.memset(ap, val), [ap], [])

    def scan(self, out, d0, d1, init=0.0):
        self._op("dve", lambda e: e.tensor_tensor_scan(out, d0, d1, init, ALU.mult, ALU.add), [out], [d0, d1])

    def recip(self, out, in_):
        self._op("dve", lambda e: e.reciprocal(out, in_), [out], [in_])

    def red(self, out, in_):
        self._op("dve", lambda e: e.tensor_reduce(out, in_, mybir.AxisListType.X, ALU.add), [out], [in_])

    def dma(self, q, out, in_, key, slow=False):
        self._op(q, lambda e: e.dma_start(out=out, in_=in_, allow_slow_non_contiguous=slow), [out], [in_],
                 dma=True, semkey=key)

    def dbg(self, name, ap, region):
        shp = list(ap.shape)
        d = self.nc.dram_tensor("dbg_" + name, shp, ap.dtype, kind="ExternalOutput").ap()
        self.dma("sp", d, ap, key="dbg_" + name)
        self.dbg_outs.append("dbg_" + name)


DM = 2048
KT = 16
DP = 1024
NPT = 8
DFF = 6144
NFT = 48
DPLE = 256
DIN = 6144
PRE = 32
OWN = 1024
NSMP = 16
NC = 536
NT = [(0, 272), (272, 536)]
PFX = 992
PN = 496
NCHM = 67
LCH = 8
EPS = 1e-6
SEQS = [536, 520]
TWO_PI = 6.283185307179586
CW1 = 6.28125
CW2 = float(np.float32(TWO_PI - CW1))
CW3 = float(TWO_PI - CW1 - CW2)


def bc(ap, shape):
    return ap.to_broadcast(list(shape))


class Step(KB):
    def poly(self, out, x, coefs):
        n = len(coefs) - 1
        self.ts("dve", out, x, float(coefs[n]), ALU.mult)
        for k in range(n - 1, 0, -1):
            self.stt(out, out, float(coefs[k]), x, ALU.add, ALU.mult)
        self.ts("dve", out, out, float(coefs[0]), ALU.add)

    def cmul(self, eng, o_re, o_im, a_re, a_im, b_re, b_im, t1, t2, t3=None, t4=None):
        e1, e2 = (eng, eng) if eng != "split" else ("dve", "pool")
        if eng != "split":
            t3, t4 = t1, t2
        self.tt(e1, t1, a_re, b_re, ALU.mult)
        self.tt(e1, t2, a_im, b_im, ALU.mult)
        if eng == "split":
            self.tt(e2, t3, a_re, b_im, ALU.mult)
            self.tt(e2, t4, a_im, b_re, ALU.mult)
            self.tt(e1, o_re, t1, t2, ALU.subtract)
            self.tt(e2, o_im, t3, t4, ALU.add)
        else:
            self.tt(e1, o_re, t1, t2, ALU.subtract)
            self.tt(e1, t1, a_re, b_im, ALU.mult)
            self.tt(e1, t2, a_im, b_re, ALU.mult)
            self.tt(e1, o_im, t1, t2, ALU.add)

    def emit_setup(self, din, pers, reg, dbg=False):
        A = lambda name, shape, dt=F32: self.alloc(reg, name, shape, dt)
        PA = lambda name, shape, dt=F32: self.alloc(pers, name, shape, dt)
        nc = self.nc
        self.P.bulk_keys.add("su")
        LR, LI, LDT = A("LR", [128, 32]), A("LI", [128, 32]), A("LDT", [128, 32])
        self.dma("sp", LR, din["lamv"][:, 0, :], "su")
        self.dma("sp", LI, din["lamv"][:, 1, :], "su")
        self.dma("sp", LDT, din["lamv"][:, 2, :], "su")
        mask = A("mask", [128, 128])
        if getattr(self, "ident", None) is None:
            self.ident = PA("ident", [128, 128])
            self.dma("sp", self.ident, din["ident"], "su")
        ident = self.ident
        self.dma("sp", mask, din["mask"], "su")
        dsk = self.dsk
        Bz = [A("Bz%d" % r, [128, 32, 32]) for r in range(2)]
        for r, nm in enumerate(("b_re", "b_im")):
            self.ms("dve", Bz[r], 0.0)
            bv = din[nm].rearrange("g p h -> (g p) h").rearrange("(j q) h -> q j h", q=128)
            for g2 in range(2):
                self.dma("sp", Bz[r][64 * g2:64 * g2 + 64, :, 16 * g2:16 * g2 + 16], bv[64 * g2:64 * g2 + 64], "su")
        Cz = [A("Cz%d" % r, [128, 32, 32]) for r in range(2)]
        C2 = [A("C2_%d" % k, [128, 128]) for k in range(2)]
        n_tr = 0
        for r, nm in enumerate(("c_re", "c_im")):
            self.ms("dve", Cz[r], 0.0)
            cv = din[nm].rearrange("g h p -> (g h) p")
            for i in range(8):
                c2 = C2[n_tr % 2]
                for hh in range(2):
                    self.dma("sp", c2[:, 64 * hh:64 * hh + 64], cv[128 * i:128 * i + 128, :], "suc%d_%d" % (n_tr % 2, hh))
                pst = self.ps[n_tr % 2]
                self.tr(pst[:, 0:128], c2, ident)
                for g2 in range(2):
                    src = pst[64 * g2:64 * g2 + 64, 0:128].rearrange("p (q a h) -> p q a h", q=4, a=2)[:, :, g2, :]
                    self.cp("dve", Cz[r][64 * g2:64 * g2 + 64, 4 * i:4 * i + 4, 16 * g2:16 * g2 + 16], src)
                n_tr += 1
        T = lambda nm: A(nm, [128, 32])
        dt_, X, TH, mag = T("dt"), T("X"), T("TH"), T("mag")
        fact = [1.0]
        for k in range(1, 20):
            fact.append(fact[-1] * k)
        self.ts("dve", LDT, LDT, 0.125, ALU.mult)
        self.poly(dt_, LDT, [1.0 / fact[k] for k in range(14)])
        for _ in range(3):
            self.tt("dve", dt_, dt_, dt_, ALU.mult)
        self.tt("dve", X, LR, dt_, ALU.mult)
        self.tt("dve", TH, LI, dt_, ALU.mult)
        self.poly(mag, X, [1.0 / fact[k] for k in range(8)])
        kf, ki, r4, z, sn, cs, t1, t2 = T("kf"), A("ki", [128, 32], I32), T("r4"), T("z"), T("sn"), T("cs"), T("t1"), T("t2")
        self.ts("dve", kf, TH, 1.0 / TWO_PI, ALU.mult)
        self.cp("dve", ki, kf)
        self.cp("dve", kf, ki)
        self.stt(r4, kf, -CW1, TH, ALU.mult, ALU.add)
        self.stt(r4, kf, -CW2, r4, ALU.mult, ALU.add)
        self.stt(r4, kf, -CW3, r4, ALU.mult, ALU.add)
        self.ts("dve", r4, r4, 0.25, ALU.mult)
        self.tt("dve", z, r4, r4, ALU.mult)
        self.poly(sn, z, [((-1.0) ** k) / fact[2 * k + 1] for k in range(8)])
        self.tt("dve", sn, sn, r4, ALU.mult)
        self.poly(cs, z, [((-1.0) ** k) / fact[2 * k] for k in range(9)])
        for _ in range(2):
            self.stt(t1, sn, 2.0, cs, ALU.mult, ALU.mult)
            self.tt("dve", t2, sn, sn, ALU.mult)
            self.ts("dve", cs, t2, -2.0, ALU.mult, 1.0, ALU.add)
            self.cp("dve", sn, t1)
        ar, ai = PA("ar", [128, 32]), PA("ai", [128, 32])
        self.tt("dve", ar, mag, cs, ALU.mult)
        self.tt("dve", ai, mag, sn, ALU.mult)
        self.ar1, self.ai1 = ar, ai
        nr, den, cr, ci = T("nr"), T("den"), T("cr"), T("ci")
        self.ts("dve", nr, ar, -1.0, ALU.add)
        self.tt("dve", t1, LR, LR, ALU.mult)
        self.tt("dve", t2, LI, LI, ALU.mult)
        self.tt("dve", den, t1, t2, ALU.add)
        self.recip(den, den)
        self.tt("dve", t1, nr, LR, ALU.mult)
        self.tt("dve", t2, ai, LI, ALU.mult)
        self.tt("dve", cr, t1, t2, ALU.add)
        self.tt("dve", cr, cr, den, ALU.mult)
        self.tt("dve", t1, ai, LR, ALU.mult)
        self.tt("dve", t2, nr, LI, ALU.mult)
        self.tt("dve", ci, t1, t2, ALU.subtract)
        self.tt("dve", ci, ci, den, ALU.mult)
        PWr, PWi = A("PWr", [128, 9, 32]), A("PWi", [128, 9, 32])
        self.ms("dve", PWr[:, 0, :], 1.0)
        self.ms("dve", PWi[:, 0, :], 0.0)
        self.cp("dve", PWr[:, 1, :], ar)
        self.cp("dve", PWi[:, 1, :], ai)
        for p in range(2, 9):
            self.cmul("dve", PWr[:, p, :], PWi[:, p, :], PWr[:, p - 1, :], PWi[:, p - 1, :], ar, ai, t1, t2)
        r8, u8c, u8s = PA("r8", [128, 32]), PA("u8c", [128, 32]), PA("u8s", [128, 32])
        self.tt("dve", r8, mag, mag, ALU.mult)
        self.tt("dve", r8, r8, r8, ALU.mult)
        self.tt("dve", r8, r8, r8, ALU.mult)
        wc, ws = T("wc"), T("ws")
        self.cp("dve", wc, cs)
        self.cp("dve", ws, sn)
        for _ in range(3):
            self.cmul("dve", z, kf, wc, ws, wc, ws, t1, t2)
            self.cp("dve", wc, z)
            self.cp("dve", ws, kf)
        self.cp("dve", u8c, wc)
        self.cp("dve", u8s, ws)
        self.r8, self.u8c, self.u8s = r8, u8c, u8s
        ETc, ETs = A("ETc", [128, 32, NCHM]), A("ETs", [128, 32, NCHM])
        X1, X2 = A("X1", [128, 32, 32]), A("X2", [128, 32, 32])
        BB = [A("BB%d" % r, [128, 32, 32]) for r in range(2)]
        crb, cib = bc(cr.unsqueeze(2), [128, 32, 32]), bc(ci.unsqueeze(2), [128, 32, 32])
        self.cmul("dve", BB[0], BB[1], Bz[0], Bz[1], crb, cib, X1, X2)
        X3 = self.alloc(Region(*self.allocs[Bz[0].name]), "X3", [128, 32, 32])
        X4 = self.alloc(Region(*self.allocs[Bz[1].name]), "X4", [128, 32, 32])
        self.ms("dve", ETc[:, :, 0:1], 1.0)
        self.ms("dve", ETs[:, :, 0:1], 0.0)
        s = 1
        while s < NCHM:
            m = min(s, NCHM - s)
            wcb, wsb = bc(wc.unsqueeze(2), [128, 32, m]), bc(ws.unsqueeze(2), [128, 32, m])
            self.cmul("dve", ETc[:, :, s:s + m], ETs[:, :, s:s + m], ETc[:, :, 0:m], ETs[:, :, 0:m], wcb, wsb,
                      X1[:, :, 0:m], X2[:, :, 0:m], X3[:, :, 0:m], X4[:, :, 0:m])
            s *= 2
            if s < NCHM:
                self.cmul("dve", z, kf, wc, ws, wc, ws, t1, t2)
                self.cp("dve", wc, z)
                self.cp("dve", ws, kf)
        yield "etables"
        TAB_d = nc.dram_tensor("TAB_d", [2, 128, 32, NCHM], F32, kind="Internal").ap()
        self.P.bulk_keys.add("tab")
        for t, src in enumerate((ETc, ETs)):
            self.dma("sp", TAB_d[t], src, "tab")
        self.TAB_d = TAB_d
        QL = PN // LCH + 1
        Qr = self.alloc(Region(*self.allocs[ETc.name]), "Qr", [128, 32, QL])
        Qi = self.alloc(Region(*self.allocs[ETs.name]), "Qi", [128, 32, QL])
        self.ms("dve", Qr[:, :, QL - 1:QL], 1.0)
        self.ms("dve", Qi[:, :, QL - 1:QL], 0.0)
        self.cp("dve", wc, PWr[:, 8, :])
        self.cp("dve", ws, PWi[:, 8, :])
        s = 1
        while s < QL:
            m = min(s, QL - s)
            wcb, wsb = bc(wc.unsqueeze(2), [128, 32, m]), bc(ws.unsqueeze(2), [128, 32, m])
            self.cmul("dve", Qr[:, :, QL - s - m:QL - s], Qi[:, :, QL - s - m:QL - s], Qr[:, :, QL - m:QL], Qi[:, :, QL - m:QL],
                      wcb, wsb, X1[:, :, 0:m], X2[:, :, 0:m], X3[:, :, 0:m], X4[:, :, 0:m])
            s *= 2
            if s < QL:
                self.cmul("dve", z, kf, wc, ws, wc, ws, t1, t2)
                self.cp("dve", wc, z)
                self.cp("dve", ws, kf)
        PQ_d = self.nc.dram_tensor("PQ_d", [2, 128, 32, QL], F32, kind="Internal").ap()
        self.P.bulk_keys.add("tabq")
        self.dma("sp", PQ_d[0], Qr, "tabq")
        self.dma("sp", PQ_d[1], Qi, "tabq")
        self.PQ_d, self.QL = PQ_d, QL
        yield "tables"
        WB_d = nc.dram_tensor("WB_d", [128, 8, 8, 2, 128], BF16, kind="Internal").ap()
        WC_d = nc.dram_tensor("WC_d", [128, 32, 8, 2, 32], BF16, kind="Internal").ap()
        KW_d = nc.dram_tensor("KW_d", [128, 8, 8, 128], BF16, kind="Internal").ap()
        self.WB_d, self.WC_d, self.KW_d = WB_d, WC_d, KW_d
        NCz = A("NCz", [128, 32, 32])
        self.ts("dve", NCz, Cz[1], -1.0, ALU.mult)
        rv = [Region(*self.allocs[ETc.name]), Region(*self.allocs[ETs.name])]
        V = [[self.alloc(rv[r], "V%d_%d" % (k, r), [128, 32, 32]) for r in range(2)] for k in range(2)]
        WBS = [A("WBS0", [128, 8, 2, 128], BF16)] * 2
        KWS = [A("KWS0", [128, 8, 128], BF16)] * 2
        WCS = [A("WCS0", [128, 32, 2, 32], BF16)] * 2
        tmpK = A("tmpK", [128, 128])
        def wc_part(tau):
            prb = bc(PWr[:, tau, :].unsqueeze(2), [128, 32, 32])
            pib = bc(PWi[:, tau, :].unsqueeze(2), [128, 32, 32])
            l = tau - 1
            wcs = WCS[tau % 2]
            self.tt("dve", X1, Cz[0], prb, ALU.mult)
            self.tt("dve", X2, Cz[1], pib, ALU.mult)
            self.tt("dve", X3, NCz, prb, ALU.mult)
            self.tt("dve", X4, Cz[0], pib, ALU.mult)
            self.tt("dve", wcs[:, :, 0, :], X1, X2, ALU.subtract)
            self.tt("dve", wcs[:, :, 1, :], X3, X4, ALU.subtract)
            self.dma("sp", WC_d[:, :, l, :, :], wcs, "wcs0")

        for tau in range(9):
            prb = bc(PWr[:, tau, :].unsqueeze(2), [128, 32, 32])
            pib = bc(PWi[:, tau, :].unsqueeze(2), [128, 32, 32])
            if tau <= 7:
                Vr, Vi = V[tau % 2]
                self.cmul("dve", Vr, Vi, BB[0], BB[1], prb, pib, X1, X2)
                l = 7 - tau
                wbs, kws = WBS[tau % 2], KWS[tau % 2]
                Vv = [Vr.rearrange("p j c -> p (j c)").rearrange("p (i c) -> p i c", i=8),
                      Vi.rearrange("p j c -> p (j c)").rearrange("p (i c) -> p i c", i=8)]
                Czv = [Cz[0].rearrange("p j c -> p (j c)").rearrange("p (i c) -> p i c", i=8),
                       NCz.rearrange("p j c -> p (j c)").rearrange("p (i c) -> p i c", i=8)]
                for i in range(8):
                    pst = self.ps[2 + (i % 2)]
                    for r in range(2):
                        self.tr(pst[:, 128 * r:128 * r + 128], Vv[r][:, i, :], ident)
                    self.cp("act", wbs[:, i, :, :], pst[:, 0:256].rearrange("p (r c) -> p r c", r=2))
                    psk = self.ps[4 + (i % 4)]
                    self.mm(psk[:, 0:128], lhsT=Vv[0][:, i, :], rhs=Czv[0][:, i, :], start=True, stop=False)
                    self.mm(psk[:, 0:128], lhsT=Vv[1][:, i, :], rhs=Czv[1][:, i, :], start=False, stop=True)
                    if i == 3 and tau >= 1:
                        wc_part(tau)
                    if tau == 0:
                        self.tt("dve", tmpK, psk[:, 0:128], mask, ALU.mult)
                        self.stt(kws[:, i, :], ident, dsk[:, i:i + 1], tmpK, ALU.mult, ALU.add)
                    else:
                        self.tt("dve", kws[:, i, :], psk[:, 0:128], mask, ALU.mult)
                self.dma("sp", WB_d[:, :, l, :, :], wbs, "wbs0")
                self.dma("sp", KW_d[:, :, tau, :], kws, "kws0")
            else:
                wc_part(tau)
            if tau == 3:
                yield "half"
        if dbg:
            for nm, ap in (("ar", ar), ("ai", ai), ("cr", cr), ("ci", ci), ("r8", r8), ("u8c", u8c), ("u8s", u8s),
                           ("ETc", ETc), ("ETs", ETs)):
                self.dbg(nm, ap, reg)

    def init_main(self, din, pers):
        self.din = din
        PA = lambda name, shape, dt=F32: self.alloc(pers, name, shape, dt)
        self.xT = [PA("xT%d" % k, [128, NC]) for k in range(KT)]
        self.hT = [PA("hT%d" % k, [128, NC], F32R) for k in range(KT)]
        self.NW = 4
        self.wslot = [PA("W%d" % k, [128, 2048], F32R) for k in range(self.NW)]
        self.wi = 0
        self.ident = PA("ident", [128, 128])
        self.dma("sp", self.ident, din["ident"], "su0")
        self.ones = PA("ones", [128, 128], F32R)
        self.dma("sp", self.ones, din["ones"], "su0")
        self.gcol = {}
        vecs = PA("vecs", [128, 272])
        self.dma("sp", vecs, din["vecs"], "su0")
        for k, nm in enumerate(("g_mix", "g_ffn", "g_ple", "g_final")):
            self.gcol[nm] = vecs[:, 16 * k:16 * k + 16]
        self.pscale = vecs[:, 64:72]
        self.dsk = vecs[:, 72:80]
        self.wconv = vecs[:, 80:224].rearrange("p (j t) -> p j t", j=3)
        self.bconv = vecs[:, 224:272]
        self.invc = PA("invc", [128, 4, 16])
        self.dma("sp", self.invc, din["invc"], "su0")
        self.sq = [PA("sq%d" % k, [128, NC], F32R) for k in range(2)]
        self.rstd = PA("rstd", [128, NC])
        self.sig = [PA("sig%d" % k, [128, 272]) for k in range(4)]
        self.sgi = 0
        self.TAILP = PA("TAILP", [128, NPT, 16])
        self.ms("dve", self.TAILP, 0.0)
        self.CT = PA("CT", [128, NFT, 2])
        self.ms("dve", self.CT, 0.0)
        self.Hc = PA("Hc", [128, 2, 32])
        self.ms("dve", self.Hc, 0.0)
        self.bank = 0
        self.xti = 0
        self.xcopy_act = False

    def next_sig(self):
        t = self.sig[self.sgi % 4]
        self.sgi += 1
        return t

    def wload(self, view, kt, mc):
        k = self.wi % self.NW
        self.wi += 1
        ap = self.wslot[k][:, 0:kt * mc].rearrange("p (k c) -> p k c", k=kt)
        self.dma("sp", ap, view, "w%d" % k)
        return ap

    def wview(self, name, c0, mc, k0=0, kt=None):
        w = self.din[name]
        v = w.rearrange("(k p) c -> p k c", p=128)
        if kt is None:
            kt = v.shape[1] - k0
        return v[:, k0:k0 + kt, c0:c0 + mc]

    def mm_acc(self, psb, wblk, cs, rhs, kt, nts):
        for n, (a, b) in enumerate(nts):
            for k in range(kt):
                self.mm(self.ps[psb[n]][:, 0:b - a], lhsT=wblk[:, k, cs:cs + 128], rhs=rhs[k][:, a:b],
                        start=(k == 0), stop=(k == kt - 1))

    def load_xT(self, xrows, ncols, col0=0, dst=None, stg=None):
        t0 = 0
        dst = dst or self.xT
        stg = stg or self.xtok
        while t0 < ncols:
            nt = min(128, ncols - t0)
            xt = stg[self.xti % 2]
            self.xti += 1
            self.dma("act", xt[0:nt, :], xrows[t0:t0 + nt, :], "xin%d" % (self.xti % 2))
            for kg in range(4):
                pst = self.ps[self.bank % 8]
                self.bank += 1
                for kk in range(4):
                    k = 4 * kg + kk
                    self.tr(pst[:, 128 * kk:128 * kk + nt], xt[0:nt, 128 * k:128 * k + 128], self.ident[0:nt, 0:nt])
                for kk in range(4):
                    k = 4 * kg + kk
                    eng = "act" if (kk % 2 == 0 or self.xcopy_act) else "dve"
                    self.cp(eng, dst[k][:, col0 + t0:col0 + t0 + nt], pst[:, 128 * kk:128 * kk + nt])
            t0 += nt

    def rmsnorm(self, gname, nts, ncols, out=None):
        out = out or self.hT
        g = self.gcol[gname]
        b0 = self.bank
        self.bank += 2
        for k in range(KT):
            sq = self.sq[k % 2]
            self.act(sq[:, 0:ncols], self.xT[k][:, 0:ncols], AF.Square)
            for n, (a, b) in enumerate(nts):
                self.mm(self.ps[(b0 + n) % 8][:, 0:b - a], lhsT=self.ones, rhs=sq[:, a:b], start=(k == 0), stop=(k == KT - 1))
        for n, (a, b) in enumerate(nts):
            self.act(self.rstd[:, a:b], self.ps[(b0 + n) % 8][:, 0:b - a], AF.Sqrt, scale=1.0 / DM, bias=EPS)
        self.recip(self.rstd[:, 0:ncols], self.rstd[:, 0:ncols])
        for k in range(KT):
            self.stt(out[k][:, 0:ncols], self.xT[k][:, 0:ncols], g[:, k:k + 1], self.rstd[:, 0:ncols], ALU.mult, ALU.mult)

    def s5_alloc(self, reg):
        A = lambda name, shape, dt=F32: self.alloc(reg, name, shape, dt)
        self.WBs = [A("WBs%d" % k, [128, 8, 2, 128], BF16) for k in range(2)]
        self.WCs = [A("WCs%d" % k, [128, 4, 8, 2, 32], BF16) for k in range(2)]
        self.KWs = [A("KWs%d" % k, [128, 8, 128], BF16) for k in range(2)]
        self.TBs = [A("TBs%d" % k, [128, 3, 4, NCHM]) for k in range(2)]
        self.Ss = [A("Ss%d" % k, [128, 2, 4, NCHM]) for k in range(2)]
        self.Hbs = [A("Hbs%d" % k, [128, 2, 4, NCHM], BF16) for k in range(2)]
        self.Mt = [A("Mt%d" % k, [128, 4, NCHM]) for k in range(6)]
        self.INI = A("INI", [128, 2, 32])
        self.tmp32b = [A("tmp32b_%d" % k, [128, 32]) for k in range(2)]
        self.tmp4 = [A("tmp4_%d" % k, [128, 4]) for k in range(2)]

    def s5_fetch_wb(self, i, slot):
        self.dma("sp", self.WBs[slot], self.WB_d[:, i], "wb%d" % slot)

    def s5_fetch_y(self, i, slot):
        self.dma("sp", self.TBs[slot][:, 0:2], self.TAB_d[:, :, 4 * i:4 * i + 4, :].rearrange("t p j c -> p t j c"), "tb%d" % slot)
        self.dma("sp", self.WCs[slot], self.WC_d[:, 4 * i:4 * i + 4], "wc%d" % slot)
        self.dma("sp", self.KWs[slot], self.KW_d[:, i], "kw%d" % slot)

    def s5_mm(self, i, U_i, n, slot):
        wb, S = self.WBs[slot], self.Ss8[i]
        for r in range(2):
            for l in range(LCH):
                for q in range(4):
                    self.mm(self.ps[4 + q][:, r * NCHM:r * NCHM + n], lhsT=wb[32 * q:32 * q + 32, l, r, :],
                            rhs=U_i[32 * q:32 * q + 32, l * n:(l + 1) * n], start=(l == 0), stop=(l == LCH - 1),
                            tp=(32 * q, 0))
        for q in range(4):
            self.cp("act", S[:, :, q, 0:n], self.ps[4 + q][:, 0:2 * NCHM].rearrange("p (r c) -> p r c", r=2)[:, :, 0:n])

    def s5_chain(self, i, n, slot):
        tb, S = self.TBs[slot], self.Ss8[i]
        Ec, Es = tb[:, 0, :, 0:n], tb[:, 1, :, 0:n]
        Sr, Si = S[:, 0, :, 0:n], S[:, 1, :, 0:n]
        M = [m[:, :, 0:n] for m in self.Mt]
        hc_r, hc_i = self.Hc[:, 0, 4 * i:4 * i + 4], self.Hc[:, 1, 4 * i:4 * i + 4]
        ini = self.INI[:, :, 4 * i:4 * i + 4]
        self.tt("dve", M[0], Ec, Sr, ALU.mult)
        self.tt("dve", M[1], Es, Si, ALU.mult)
        self.tt("dve", M[0], M[0], M[1], ALU.add)
        self.tt("dve", M[2], Ec, Si, ALU.mult)
        self.tt("dve", M[3], Es, Sr, ALU.mult)
        self.tt("dve", M[2], M[2], M[3], ALU.subtract)
        for q in range(4):
            r8b = self.r8[:, 4 * i + q:4 * i + q + 1].to_broadcast([128, n])
            self.scan(M[4][:, q, :], r8b, M[0][:, q, :], ini[:, 0, q:q + 1])
            self.scan(M[5][:, q, :], r8b, M[2][:, q, :], ini[:, 1, q:q + 1])
        Hb = self.Hbs[slot]
        self.cp("dve", Hb[:, 0, :, 0:1], hc_r.unsqueeze(2))
        self.cp("dve", Hb[:, 1, :, 0:1], hc_i.unsqueeze(2))
        self.tt("dve", M[0], Ec, M[4], ALU.mult)
        self.tt("dve", M[1], Es, M[5], ALU.mult)
        self.tt("dve", M[2], Ec, M[5], ALU.mult)
        self.tt("dve", M[3], Es, M[4], ALU.mult)
        self.tt("dve", Hb[:, 0, :, 1:n], M[0][:, :, 0:n - 1], M[1][:, :, 0:n - 1], ALU.subtract)
        self.tt("dve", Hb[:, 1, :, 1:n], M[2][:, :, 0:n - 1], M[3][:, :, 0:n - 1], ALU.add)
        self.tt("dve", hc_r, M[0][:, :, n - 1], M[1][:, :, n - 1], ALU.subtract)
        self.tt("dve", hc_i, M[2][:, :, n - 1], M[3][:, :, n - 1], ALU.add)

    def s5_prefix_fetch(self, i, slot):
        tq = self.TBs[slot]
        tqv = tq.rearrange("p t j c -> p (t j c)")[:, 0:2 * 4 * self.QL].rearrange("p (t j c) -> p t j c", t=2, j=4)
        self.dma("sp", self.WBs[slot], self.WB_d[:, i], "wb%d" % slot)
        self.dma("sp", tqv, self.PQ_d[:, :, 4 * i:4 * i + 4, :].rearrange("t p j c -> p t j c"), "tb%d" % slot)

    def s5_prefix(self, i, U_i, n, slot, HN):
        wb, tq, S = self.WBs[slot], self.TBs[slot], self.Ss[slot]
        tqv = tq.rearrange("p t j c -> p (t j c)")[:, 0:2 * 4 * self.QL].rearrange("p (t j c) -> p t j c", t=2, j=4)
        for r in range(2):
            for l in range(LCH):
                for q in range(4):
                    self.mm(self.ps[4 + q][:, r * NCHM:r * NCHM + n], lhsT=wb[32 * q:32 * q + 32, l, r, :],
                            rhs=U_i[32 * q:32 * q + 32, l * n:(l + 1) * n], start=(l == 0), stop=(l == LCH - 1),
                            tp=(32 * q, 0))
        for q in range(4):
            self.cp("act", S[:, :, q, 0:n], self.ps[4 + q][:, 0:2 * NCHM].rearrange("p (r c) -> p r c", r=2)[:, :, 0:n])
        Qr, Qi = tqv[:, 0, :, 1:n + 1], tqv[:, 1, :, 1:n + 1]
        Sr, Si = S[:, 0, :, 0:n], S[:, 1, :, 0:n]
        M = [m[:, :, 0:n] for m in self.Mt]
        self.tt("dve", M[0], Qr, Sr, ALU.mult)
        self.tt("dve", M[1], Qi, Si, ALU.mult)
        self.tt("dve", M[2], Qr, Si, ALU.mult)
        self.tt("dve", M[3], Qi, Sr, ALU.mult)
        self.tt("dve", M[0], M[0], M[1], ALU.subtract)
        self.tt("dve", M[2], M[2], M[3], ALU.add)
        self.red(HN[:, 0, 4 * i:4 * i + 4], M[0])
        self.red(HN[:, 1, 4 * i:4 * i + 4], M[2])

    def s5_y(self, i, U_i, n, slot, GY_i, seq, smp=None):
        wc, kw, Hb = self.WCs[slot], self.KWs[slot], self.Hbs[slot]
        first = [True, True]
        for l in range(LCH):
            hf, lo = l // 4, (l % 4) * n
            for lp in range(l + 1):
                self.mm(self.ps[hf][:, lo:lo + n], lhsT=kw[:, l - lp, :], rhs=U_i[:, lp * n:(lp + 1) * n],
                        start=first[hf], stop=False)
                first[hf] = False
            for q in range(4):
                for r in range(2):
                    last = (l % 4 == 3) and r == 1
                    self.mm(self.ps[hf][32 * q:32 * q + 32, lo:lo + n], lhsT=wc[:, q, l, r, :], rhs=Hb[:, r, q, 0:n],
                            start=False, stop=last, tp=(0, 32 * q))
        for hf in range(2):
            o = GY_i[:, 0:seq].rearrange("p (c l) -> p l c", l=LCH)[:, 4 * hf:4 * hf + 4, :]
            self.act(o, self.ps[hf][:, 0:4 * n].rearrange("p (l c) -> p l c", l=4), AF.Gelu_apprx_tanh)
        if smp is not None:
            Us, H0b = smp
            self.mm(self.ps[2][:, 0:NSMP], lhsT=kw[:, 0, :], rhs=Us, start=True, stop=False)
            for q in range(4):
                for r in range(2):
                    self.mm(self.ps[2][32 * q:32 * q + 32, 0:NSMP], lhsT=wc[:, q, 0, r, :], rhs=H0b[:, r, 4 * i + q, :],
                            start=False, stop=(r == 1), tp=(0, 32 * q))
            self.act(GY_i[:, seq:seq + NSMP], self.ps[2][:, 0:NSMP], AF.Gelu_apprx_tanh)

    def prefix_front(self, pp, U, load=True):
        nts = [(0, PN)]
        if load:
            self.load_xT(self.din["xcat"][pp * PN:(pp + 1) * PN, :], PN)
        self.rmsnorm("g_mix", nts, PN)
        for i in range(NPT):
            wblk = self.wload(self.wview("w_in", DP + 128 * i, 128), KT, 128)
            b = self.bank % 8
            self.bank += 1
            self.mm_acc([b], wblk, 0, self.hT, KT, nts)
            self.cp("act", U[i][:, 0:PN].rearrange("p (l c) -> p c l", l=LCH),
                    self.ps[b][:, 0:PN].rearrange("p (c l) -> p c l", l=LCH))

    def prefix_s5(self, pp, U, HN, bufs):
        SP, QT, T = bufs
        n = PN // LCH
        for k in range(2):
            self.dma("sp", self.WBs[k], self.WB_d[:, k], "wb%d" % k)
        for i in range(NPT):
            wb = self.WBs[i % 2]
            for r in range(2):
                for l in range(LCH):
                    for q in range(4):
                        self.mm(self.ps[4 + q][:, r * NCHM:r * NCHM + n], lhsT=wb[32 * q:32 * q + 32, l, r, :],
                                rhs=U[i][32 * q:32 * q + 32, l * n:(l + 1) * n], start=(l == 0), stop=(l == LCH - 1),
                                tp=(32 * q, 0))
            for q in range(4):
                self.cp("act", SP[:, :, 4 * i + q, :],
                        self.ps[4 + q][:, 0:2 * NCHM].rearrange("p (r c) -> p r c", r=2)[:, :, 0:n])
            if i + 2 < NPT:
                self.dma("sp", self.WBs[i % 2], self.WB_d[:, i + 2], "wb%d" % (i % 2))
        Qr, Qi = QT[:, 0, :, 1:n + 1], QT[:, 1, :, 1:n + 1]
        Sr, Si = SP[:, 0], SP[:, 1]
        self.tt("dve", T[0], Qr, Sr, ALU.mult)
        self.tt("dve", T[1], Qi, Si, ALU.mult)
        self.tt("dve", T[2], Qr, Si, ALU.mult)
        self.tt("dve", T[3], Qi, Sr, ALU.mult)
        self.tt("dve", T[0], T[0], T[1], ALU.subtract)
        self.tt("dve", T[2], T[2], T[3], ALU.add)
        self.red(HN[:, 0, :], T[0])
        self.red(HN[:, 1, :], T[2])
        t1, t2 = self.tmp32
        q0r, q0i = QT[:, 0, :, 0], QT[:, 1, :, 0]
        hr, hi = self.Hc[:, 0, :], self.Hc[:, 1, :]
        self.tt("dve", t1, q0r, hr, ALU.mult)
        self.tt("dve", t2, q0i, hi, ALU.mult)
        self.tt("dve", t1, t1, t2, ALU.subtract)
        self.tt("dve", t2, q0r, hi, ALU.mult)
        self.tt("dve", hi, q0i, hr, ALU.mult)
        self.tt("dve", hi, hi, t2, ALU.add)
        self.tt("dve", hi, hi, HN[:, 1, :], ALU.add)
        self.tt("dve", hr, t1, HN[:, 0, :], ALU.add)

    def emit_chunk(self, ci, R0, R1):
        din = self.din
        seq = SEQS[ci]
        nsmp = NC - seq
        n = seq // LCH
        nts = NT
        col0 = PFX + ci * NC
        reg = Region(R0, R1)
        A = lambda name, shape, dt=F32: self.alloc(reg, name, shape, dt)
        UP = [A("UP%d" % k, [128, 16 + NC]) for k in range(NPT)]
        GY = [A("GY%d" % k, [128, NC], F32R) for k in range(NPT)]
        ypoff = reg.cur
        YP = [A("YP%d" % k, [128, NC], F32R) for k in range(NPT)]
        ypend = reg.cur
        uoff = reg.cur
        U = [A("U%d" % k, [128, NC], BF16) for k in range(NPT)]
        s5off = reg.cur
        self.s5_alloc(reg)
        ry = Region(uoff, R1)
        Y2 = [self.alloc(ry, "Y2_%d" % k, [128, NC], F32R) for k in range(NPT)]
        self.s5off, self.R1 = ry.cur, R1
        DF = [self.alloc(Region(self.allocs[GY[k].name][0], self.allocs[GY[k].name][1]), "DF%d" % k, [128, NC], F32R) for k in range(NPT)]
        rx = Region(ypoff, ypend)
        self.xtok = [self.alloc(rx, "xtok%d" % k, [128, 2048]) for k in range(2)]
        rs = Region(ypoff, ypend)
        self.Ss8 = list(self.Ss) + [self.alloc(rs, "Ss8_%d" % k, [128, 2, 4, NCHM]) for k in range(6)]
        rm = Region(self.s5off, R1)
        MG = [self.alloc(rm, "MG%d" % k, [128, NC], F32R) for k in range(8)]
        if ci == 0 and getattr(self, "chunk0_preloaded", False):
            pass
        else:
            if getattr(self, "XB", None) is not None:
                for k in range(KT):
                    self.cp(("act", "dve")[k % 2], self.xT[k], self.XB[k])
                self.XB = None
            elif not (ci == 0 and getattr(self, "chunk0_loaded", False)):
                self.load_xT(din["xcat"][col0:col0 + NC, :], NC)
            self.rmsnorm("g_mix", nts, NC)
        for k in range(NPT):
            self.cp("dve", UP[k][:, 0:16], self.TAILP[:, k, :])
        for m in range(2 * NPT):
            wblk = self.wload(self.wview("w_in", 128 * m, 128), KT, 128)
            b = self.bank % 8
            self.bank += 2
            self.mm_acc([b, (b + 1) % 8], wblk, 0, self.hT, KT, nts)
            for nn, (a, e) in enumerate(nts):
                pst = self.ps[(b + nn) % 8]
                if m < NPT:
                    self.cp("act", UP[m][:, 16 + a:16 + e], pst[:, 0:e - a])
                else:
                    i = m - NPT
                    se = min(e, seq)
                    self.cp("act", U[i][:, 0:seq].rearrange("p (l c) -> p c l", l=LCH)[:, a // LCH:se // LCH, :],
                            pst[:, 0:se - a].rearrange("p (c l) -> p c l", l=LCH))
                    if e > seq:
                        self.cp("act", U[i][:, seq:NC], pst[:, seq - a:e - a])
        smp = None
        if nsmp:
            smp_state = self.sample_state_in(reg)
        self.cmul("dve", self.INI[:, 0, :], self.INI[:, 1, :], self.Hc[:, 0, :], self.Hc[:, 1, :], self.u8c, self.u8s,
                  self.tmp32b[0], self.tmp32b[1])
        for k in range(2):
            self.s5_fetch_wb(k, k)
            self.s5_fetch_y(k, k)
        for i in range(NPT):
            self.s5_mm(i, U[i], n, i % 2)
            if i + 2 < NPT:
                self.s5_fetch_wb(i + 2, i % 2)
        for i in range(NPT):
            self.s5_chain(i, n, i % 2)
            sm = (U[i][:, seq:NC], self.H0b) if nsmp else None
            self.s5_y(i, U[i], n, i % 2, GY[i], seq, sm)
            if i + 2 < NPT:
                self.s5_fetch_y(i + 2, i % 2)
        if nsmp:
            self.sample_state_out(U, seq)
        for m in range(NPT):
            wa = self.wload(self.wview("w_glu", 128 * m, 128), NPT, 128)
            wb = self.wload(self.wview("w_glu", DP + 128 * m, 128), NPT, 128)
            b = self.bank % 8
            self.bank += 4
            self.mm_acc([b, (b + 1) % 8], wa, 0, GY, NPT, nts)
            self.mm_acc([(b + 2) % 8, (b + 3) % 8], wb, 0, GY, NPT, nts)
            for nn, (a, e) in enumerate(nts):
                sg = self.next_sig()
                self.act(sg[:, 0:e - a], self.ps[(b + 2 + nn) % 8][:, 0:e - a], AF.Sigmoid)
                self.tt("dve", Y2[m][:, a:e], self.ps[(b + nn) % 8][:, 0:e - a], sg[:, 0:e - a], ALU.mult)
        self.pool_mixer(ci, UP, DF, YP, seq, nsmp, reg)
        for mg in range(4):
            for mm_ in range(4):
                m = 4 * mg + mm_
                wgp = self.wload(self.wview("w_in", 2 * DP + 128 * m, 128), KT, 128)
                wgs = self.wload(self.wview("w_in", 2 * DP + DM + 128 * m, 128), KT, 128)
                wbp = self.wload(self.wview("w_branch_pool", 128 * m, 128), NPT, 128)
                wbs = self.wload(self.wview("w_branch_ssm", 128 * m, 128), NPT, 128)
                for nn, (a, e) in enumerate(nts):
                    b = 4 * (nn % 2)
                    w = e - a
                    self.mm_acc([b], wgp, 0, self.hT, KT, [(a, e)])
                    self.mm_acc([b + 1], wgs, 0, self.hT, KT, [(a, e)])
                    self.mm_acc([b + 2], wbp, 0, YP, NPT, [(a, e)])
                    self.mm_acc([b + 3], wbs, 0, Y2, NPT, [(a, e)])
                    s1, s2 = self.next_sig(), self.next_sig()
                    self.act(s1[:, 0:w], self.ps[b][:, 0:w], AF.Sigmoid)
                    self.act(s2[:, 0:w], self.ps[b + 1][:, 0:w], AF.Sigmoid)
                    self.tt("dve", s1[:, 0:w], s1[:, 0:w], self.ps[b + 2][:, 0:w], ALU.mult)
                    self.tt("dve", s2[:, 0:w], s2[:, 0:w], self.ps[b + 3][:, 0:w], ALU.mult)
                    self.tt("dve", MG[4 * (mg % 2) + mm_][:, a:e], s1[:, 0:w], s2[:, 0:w], ALU.add)
            mgt = MG[4 * (mg % 2):4 * (mg % 2) + 4]
            for mo4 in range(4):
                wblk = self.wload(self.wview("w_out", 512 * mo4, 512, k0=4 * mg, kt=4), 4, 512)
                for mo in range(4):
                    b = self.bank % 8
                    self.bank += 2
                    self.mm_acc([b, (b + 1) % 8], wblk, 128 * mo, mgt, 4, nts)
                    for nn, (a, e) in enumerate(nts):
                        xt = self.xT[4 * mo4 + mo][:, a:e]
                        self.tt("dve", xt, xt, self.ps[(b + nn) % 8][:, 0:e - a], ALU.add)
        self.ffn(ci, seq, nsmp, R0, R1)
        self.ple_and_out(ci, seq, nsmp, R0, R1)

    def sample_state_in(self, reg):
        self.H0f = self.alloc(reg, "H0f", [128, 2, 32, NSMP])
        self.H0b = self.alloc(reg, "H0b", [128, 2, 32, NSMP], BF16)
        for r, nm in enumerate(("st_re", "st_im")):
            for hh in range(2):
                st = self.xtok[hh]
                self.dma("act", st[0:NSMP, :], self.din[nm][:, 2048 * hh:2048 * hh + 2048], "xin%d" % hh)
                b = self.bank % 8
                self.bank += 1
                for jj in range(16):
                    self.tr(self.ps[b][:, NSMP * jj:NSMP * jj + NSMP], st[0:NSMP, 128 * jj:128 * jj + 128],
                            self.ident[0:NSMP, 0:NSMP])
                src = self.ps[b][:, 0:16 * NSMP].rearrange("p (j s) -> p j s", j=16)
                self.cp("act", self.H0f[:, r, 16 * hh:16 * hh + 16, :], src)
                self.cp("dve", self.H0b[:, r, 16 * hh:16 * hh + 16, :], src)

    def sample_state_out(self, U, seq):
        rm = Region(self.s5off, self.R1)
        wb7 = self.alloc(rm, "wb7", [128, 8, 2, 128], BF16)
        BU = self.alloc(rm, "BU", [128, 2, 32, NSMP])
        HN = self.alloc(rm, "HN", [128, 2, 32, NSMP])
        T1 = self.alloc(rm, "T1s", [128, 32, NSMP])
        T2 = self.alloc(rm, "T2s", [128, 32, NSMP])
        self.dma("act", wb7, self.WB_d[:, :, 7], "wb7")
        for i in range(NPT):
            for r in range(2):
                for q in range(4):
                    c0 = (i * 2 + r) * NSMP
                    self.mm(self.ps[4 + q][:, c0:c0 + NSMP], lhsT=wb7[32 * q:32 * q + 32, i, r, :],
                            rhs=U[i][32 * q:32 * q + 32, seq:NC], start=True, stop=True, tp=(32 * q, 0))
        for q in range(4):
            dst = BU.rearrange("p r (i q) s -> p q i r s", q=4)[:, q]
            self.cp("act", dst, self.ps[4 + q][:, 0:16 * NSMP].rearrange("p (i r s) -> p i r s", i=8, r=2))
        arb = bc(self.ar1.unsqueeze(2), [128, 32, NSMP])
        aib = bc(self.ai1.unsqueeze(2), [128, 32, NSMP])
        self.cmul("dve", HN[:, 0], HN[:, 1], self.H0f[:, 0], self.H0f[:, 1], arb, aib, T1, T2)
        self.tt("dve", HN, HN, BU, ALU.add)
        for r, nm in enumerate(("o_sre", "o_sim")):
            for hh in range(2):
                st = self.xtok[hh]
                for g4 in range(4):
                    b = self.bank % 8
                    self.bank += 1
                    for jj in range(4):
                        j = 16 * hh + 4 * g4 + jj
                        self.tr(self.ps[b][0:NSMP, 128 * jj:128 * jj + 128], HN[:, r, j, :], self.ident)
                    self.cp("act", st[0:NSMP, 512 * g4:512 * g4 + 512], self.ps[b][0:NSMP, 0:512])
                self.dma("act", self.din[nm][:, 2048 * hh:2048 * hh + 2048], st[0:NSMP, :], "xout%d" % hh)

    def pool_mixer(self, ci, UP, DF, YP, seq, nsmp, reg):
        rm = Region(self.s5off, self.R1)
        TA = [self.alloc(rm, "TA%d" % k, [128, 16 + NC]) for k in range(2)]
        TB = [self.alloc(rm, "TB%d" % k, [128, 16 + NC]) for k in range(2)]
        tq = self.alloc(rm, "tq", [128, 16])
        E = 16 + seq
        if nsmp:
            CPT = [self.alloc(rm, "CPT%d" % k, [128, 240]) for k in range(NPT)]
            cprow = self.din["cpool"].rearrange("s r c -> (s r) c")
            xt = self.xtok[0]
            self.dma("act", xt[:, 0:1024], cprow[0:128, :], "xin0")
            self.dma("act", xt[0:112, 1024:2048], cprow[128:240, :], "xin1")
            for m in range(NPT):
                b = self.bank % 8
                self.bank += 1
                self.tr(self.ps[b][:, 0:128], xt[:, 128 * m:128 * m + 128], self.ident)
                self.tr(self.ps[b][:, 128:240], xt[0:112, 1024 + 128 * m:1024 + 128 * m + 128], self.ident[0:112, 0:112])
                self.cp("act", CPT[m], self.ps[b][:, 0:240])
        wblk = self.wload(self.din["w_pool"].rearrange("g (kk p) c -> p (g kk) c", p=128), 8, 256)
        for k in range(4):
            w = 2 << k
            for mm_ in range(2):
                m = 2 * k + mm_
                x = UP[m]
                cur = x
                sh = 1
                idx = 0
                while sh < w:
                    dst = (TA if idx % 2 == 0 else TB)[mm_]
                    self.tt("dve", dst[:, sh:E], cur[:, sh:E], cur[:, 0:E - sh], ALU.add)
                    self.cp("dve", dst[:, 0:sh], cur[:, 0:sh])
                    cur = dst
                    sh *= 2
                    idx += 1
                self.stt(DF[m][:, 0:seq], cur[:, 16:E], 1.0 / w, x[:, 16:E], ALU.mult, ALU.subtract)
                if ci == 0:
                    self.tt("dve", tq, cur[:, 16 + PRE:32 + PRE], self.invc[:, k, :], ALU.mult)
                    self.tt("dve", DF[m][:, PRE:PRE + 16], tq, x[:, 16 + PRE:32 + PRE], ALU.subtract)
                if nsmp:
                    us = x[:, E:E + NSMP]
                    self.red(tq, CPT[m].rearrange("p (s r) -> p s r", r=15)[:, :, 16 - w:15])
                    self.tt("dve", tq, tq, us, ALU.add)
                    self.stt(DF[m][:, seq:NC], tq, 1.0 / w, us, ALU.mult, ALU.subtract)
                self.cp("dve", self.TAILP[:, m, :], x[:, E - 16:E])
            for mo2 in range(2):
                b = self.bank % 8
                self.bank += 2
                self.mm_acc([b, (b + 1) % 8], wblk[:, 2 * k:2 * k + 2, :], 128 * mo2, [DF[2 * k], DF[2 * k + 1]], 2, NT)
                for nn, (a, e) in enumerate(NT):
                    self.act(YP[2 * k + mo2][:, a:e], self.ps[(b + nn) % 8][:, 0:e - a], AF.Copy,
                             scale=self.pscale[:, 2 * k + mo2:2 * k + mo2 + 1])
        if nsmp:
            stg = self.alloc(rm, "stgp", [16, 2048])
            for (c0, nr, dst) in ((E - 15, 15, self.din["o_poolp"]), (E, NSMP, self.din["o_pools"][:, 14, :])):
                for hb in range(2):
                    b = self.bank % 8
                    self.bank += 1
                    for mm_ in range(4):
                        m = 4 * hb + mm_
                        self.tr(self.ps[b][0:nr, 128 * mm_:128 * mm_ + 128], UP[m][:, c0:c0 + nr], self.ident)
                    self.cp("act", stg[0:nr, (0 if nr == 15 else 1024) + 512 * hb:(0 if nr == 15 else 1024) + 512 * hb + 512],
                            self.ps[b][0:nr, 0:512])
                o0 = 0 if nr == 15 else 1024
                self.dma("act", dst, stg[0:nr, o0:o0 + 1024], "xout%d" % (0 if nr == 15 else 1))
            self.dma("act", self.din["o_pools"][:, 0:14, :], self.din["cpool"][:, 1:15, :], "d2d0")

    def ffn(self, ci, seq, nsmp, R0, R1):
        reg = Region(R0, R1)
        A = lambda name, shape, dt=F32: self.alloc(reg, name, shape, dt)
        AS = [A("AS%d" % k, [128, 2 + NC]) for k in range(2)]
        CV = [A("CV%d" % k, [128, NC]) for k in range(2)]
        GE = [A("GE%d" % k, [128, NC]) for k in range(2)]
        ACTB = [[A("ACTB%d_%d" % (g, k), [128, NC], F32R) for k in range(6)] for g in range(2)]
        CO = A("CO", [32, DFF])
        wc = self.wconv
        XB = None
        if False and ci == 0:
            rtop = Region((R1 - KT * NC * 4 - 64) // 32 * 32, R1)
            XB = [self.alloc(rtop, "XB%d" % k, [128, NC]) for k in range(KT)]
            rco = Region(*self.allocs[CO.name])
            xstg = [self.alloc(rco, "xstg%d" % k, [128, 2048]) for k in range(2)]
        if nsmp:
            CCT = A("CCT", [128, NFT, 32])
            ASMP = A("ASMP", [128, NFT, NSMP])
            self.dma("act", CO, self.din["cconv"].rearrange("s r c -> (s r) c"), "co")
            for g in range(3):
                b = self.bank % 8
                self.bank += 1
                for jj in range(16):
                    j = 16 * g + jj
                    self.tr(self.ps[b][:, 32 * jj:32 * jj + 32], CO[0:32, 128 * j:128 * j + 128], self.ident[0:32, 0:32])
                self.cp("act", CCT[:, 16 * g:16 * g + 16, :], self.ps[b][:, 0:512].rearrange("p (j s) -> p j s", j=16))
        self.rmsnorm("g_ffn", NT, NC)
        for ga in range(8):
            acts = ACTB[ga % 2]
            if XB is not None and ga == 2:
                c1 = PFX + (ci + 1) * NC
                self.load_xT(self.din["xcat"][c1:c1 + NC, :], NC, dst=XB, stg=xstg)
                self.XB = XB
            for jj in range(6):
                j = 6 * ga + jj
                wa = self.wload(self.wview("w_up", 128 * j, 128), KT, 128)
                wv = self.wload(self.wview("w_up", DFF + 128 * j, 128), KT, 128)
                b = 4 * (j % 2)
                self.mm_acc([b, b + 1], wa, 0, self.hT, KT, NT)
                self.mm_acc([b + 2, b + 3], wv, 0, self.hT, KT, NT)
                a_s, cv, ge = AS[j % 2], CV[j % 2], GE[j % 2]
                self.cp("dve", a_s[:, 0:2], self.CT[:, j, :])
                for nn, (a, e) in enumerate(NT):
                    self.cp("act", a_s[:, 2 + a:2 + e], self.ps[b + nn][:, 0:e - a])
                self.act(cv, a_s[:, 2:2 + NC], AF.Identity, scale=wc[:, 2, j:j + 1], bias=self.bconv[:, j:j + 1])
                self.stt(cv, a_s[:, 1:1 + NC], wc[:, 1, j:j + 1], cv, ALU.mult, ALU.add)
                self.stt(cv, a_s[:, 0:NC], wc[:, 0, j:j + 1], cv, ALU.mult, ALU.add)
                if nsmp:
                    cc = CCT[:, j, :].rearrange("p (s r) -> p s r", r=2)
                    cs_ = cv[:, seq:NC]
                    self.act(cs_, a_s[:, 2 + seq:2 + NC], AF.Identity, scale=wc[:, 2, j:j + 1], bias=self.bconv[:, j:j + 1])
                    self.stt(cs_, cc[:, :, 1], wc[:, 1, j:j + 1], cs_, ALU.mult, ALU.add)
                    self.stt(cs_, cc[:, :, 0], wc[:, 0, j:j + 1], cs_, ALU.mult, ALU.add)
                    self.cp("dve", ASMP[:, j, :], a_s[:, 2 + seq:2 + NC])
                self.cp("dve", self.CT[:, j, :], a_s[:, seq:seq + 2])
                self.act(ge, cv, AF.Gelu_apprx_tanh)
                for nn, (a, e) in enumerate(NT):
                    self.tt("dve", acts[jj][:, a:e], ge[:, a:e], self.ps[b + 2 + nn][:, 0:e - a], ALU.mult)
            for mo2 in range(8):
                wblk = self.wload(self.wview("w_down", 256 * mo2, 256, k0=6 * ga, kt=6), 6, 256)
                for mo in range(2):
                    b = self.bank % 8
                    self.bank += 2
                    self.mm_acc([b, (b + 1) % 8], wblk, 128 * mo, acts, 6, NT)
                    for nn, (a, e) in enumerate(NT):
                        xt = self.xT[2 * mo2 + mo][:, a:e]
                        self.tt("dve", xt, xt, self.ps[(b + nn) % 8][:, 0:e - a], ALU.add)
        if nsmp:
            for (src_fn, nr, dst, key) in ((lambda j: self.CT[:, j, :], 2, self.din["o_convp"], "co"),
                                           (lambda j: ASMP[:, j, :], NSMP, self.din["o_convs"][:, 1, :], "co")):
                for g in range(12):
                    b = self.bank % 8
                    self.bank += 1
                    for jj in range(4):
                        j = 4 * g + jj
                        self.tr(self.ps[b][0:nr, 128 * jj:128 * jj + 128], src_fn(j), self.ident)
                    self.cp("act", CO[0:nr, 512 * g:512 * g + 512], self.ps[b][0:nr, 0:512])
                self.dma("act", dst, CO[0:nr, :], key)
            self.dma("act", self.din["o_convs"][:, 0, :], self.din["cconv"][:, 1, :], "d2d1")

    def ple_and_out(self, ci, seq, nsmp, R0, R1):
        reg = Region(R0, R1)
        A = lambda name, shape, dt=F32: self.alloc(reg, name, shape, dt)
        pT = [A("pT%d" % k, [128, NC], F32R) for k in range(2)]
        ptok = [A("ptok%d" % k, [128, DPLE]) for k in range(2)]
        OST = [A("OST%d" % k, [128, 2048]) for k in range(2)]
        prow = self.din["pcat"][ci * NC:(ci + 1) * NC, :]
        t0, it = 0, 0
        while t0 < NC:
            nt = min(128, NC - t0)
            pt = ptok[it % 2]
            self.dma("act", pt[0:nt, :], prow[t0:t0 + nt, :], "pin%d" % (it % 2))
            b = self.bank % 8
            self.bank += 1
            for kk in range(2):
                self.tr(self.ps[b][:, 128 * kk:128 * kk + nt], pt[0:nt, 128 * kk:128 * kk + 128], self.ident[0:nt, 0:nt])
                self.cp("act", pT[kk][:, t0:t0 + nt], self.ps[b][:, 128 * kk:128 * kk + nt])
            t0 += nt
            it += 1
        self.rmsnorm("g_ple", NT, NC)
        for mo in range(KT):
            wg = self.wload(self.wview("w_ple_gate", 128 * mo, 128), KT, 128)
            wp = self.wload(self.wview("w_ple", 128 * mo, 128), 2, 128)
            b = 4 * (mo % 2)
            self.mm_acc([b, b + 1], wg, 0, self.hT, KT, NT)
            self.mm_acc([b + 2, b + 3], wp, 0, pT, 2, NT)
            for nn, (a, e) in enumerate(NT):
                sg = self.next_sig()
                w = e - a
                self.act(sg[:, 0:w], self.ps[b + nn][:, 0:w], AF.Sigmoid)
                self.tt("dve", sg[:, 0:w], sg[:, 0:w], self.ps[b + 2 + nn][:, 0:w], ALU.mult)
                xt = self.xT[mo][:, a:e]
                self.tt("dve", xt, xt, sg[:, 0:w], ALU.add)
        yT = [A("yT%d" % k, [128, NC]) for k in range(KT)]
        self.rmsnorm("g_final", NT, NC, out=yT)
        c_lo = PRE if ci == 0 else 0
        row0 = 0 if ci == 0 else SEQS[0] - PRE
        t0, it = c_lo, 0
        while t0 < NC:
            nt = min(128, NC - t0)
            ost = OST[it % 2]
            for kg in range(4):
                b = self.bank % 8
                self.bank += 1
                for kk in range(4):
                    k = 4 * kg + kk
                    self.tr(self.ps[b][0:nt, 128 * kk:128 * kk + 128], yT[k][:, t0:t0 + nt], self.ident)
                self.cp("act" if kg % 2 == 0 else "dve", ost[0:nt, 512 * kg:512 * kg + 512], self.ps[b][0:nt, 0:512])
            r0 = row0 + (t0 - c_lo)
            self.dma("act", self.din["o_y"][r0:r0 + nt, :], ost[0:nt, :], "yout%d" % (it % 2))
            t0 += nt
            it += 1


W_SPECS = (("w_in", [DM, DIN]), ("w_pool", [4, 256, 256]), ("w_glu", [DP, 2 * DP]), ("w_branch_pool", [DP, DM]),
           ("w_branch_ssm", [DP, DM]), ("w_out", [DM, DM]), ("w_up", [DM, 2 * DFF]), ("w_down", [DFF, DM]),
           ("w_ple_gate", [DM, DM]), ("w_ple", [DPLE, DM]), ("ones", [128, 128]))
V_SPECS = (("xcat", [PFX + 2 * NC, DM]), ("pcat", [2 * NC, DPLE]), ("cpool", [NSMP, 15, DP]), ("st_re", [NSMP, 4096]),
           ("st_im", [NSMP, 4096]), ("cconv", [NSMP, 2, DFF]), ("invc", [128, 4, 16]), ("ident", [128, 128]),
           ("mask", [128, 128]), ("vecs", [128, 272]), ("lamv", [128, 3, 32]), ("b_re", [64, 64, 16]),
           ("b_im", [64, 64, 16]), ("c_re", [64, 16, 64]), ("c_im", [64, 16, 64]))
O_SPECS = (("o_y", [OWN + NSMP, DM]), ("o_poolp", [15, DP]), ("o_hp", [2, 128, 32]), ("o_convp", [2, DFF]),
           ("o_pools", [NSMP, 15, DP]), ("o_sre", [NSMP, 4096]), ("o_sim", [NSMP, 4096]), ("o_convs", [NSMP, 2, DFF]))


def build_program(stop_after=None, dbg=False):
    kb = Step()
    nc = kb.nc
    din = {}
    for nm, shp in W_SPECS:
        din[nm] = nc.dram_tensor(nm, shp, F32R, kind="ExternalInput").ap()
    for nm, shp in V_SPECS:
        din[nm] = nc.dram_tensor(nm, shp, F32, kind="ExternalInput").ap()
    for nm, shp in O_SPECS:
        din[nm] = nc.dram_tensor(nm, shp, F32, kind="ExternalOutput").ap()
    TOP = 229120
    pers = Region(256, TOP)
    kb.P.bulk_keys.add("su")
    kb.P.bulk_keys.add("su0")
    kb.init_main(din, pers)
    setup_pers = Region(pers.take(5376), TOP)
    setup_pers.hi = setup_pers.lo + 5376
    R0 = (setup_pers.hi + 31) // 32 * 32
    kb.R0, kb.R1 = R0, TOP
    PBUF = 2 * 8 * 992 + 8192 + 2 * 256 + 2 * 128 + 256
    ptop = Region((TOP - PBUF - 256) // 32 * 32, TOP)
    Upf = [[kb.alloc(ptop, "Upf%d_%d" % (pp, k), [128, PN], BF16) for k in range(NPT)] for pp in range(2)]
    kb.xtok = [kb.alloc(ptop, "xtokp0", [128, 2048])] * 2
    HN = kb.alloc(ptop, "HNp", [128, 2, 32])
    kb.tmp32 = [kb.alloc(ptop, "tmp32_%d" % k, [128, 32]) for k in range(2)]
    kb.stop = stop_after
    kb.xcopy_act = True
    gen = kb.emit_setup(din, setup_pers, Region(R0, ptop.lo), dbg=dbg)
    next(gen)
    kb.prefix_front(0, Upf[0])
    next(gen)
    kb.load_xT(din["xcat"][PN:2 * PN, :], PN)
    next(gen)
    kb.prefix_front(1, Upf[1], load=False)
    kb.load_xT(din["xcat"][PFX:PFX + NC, :], NC)
    kb.chunk0_loaded = True
    for _ in gen:
        pass
    kb.xcopy_act = False
    if stop_after == "setup":
        kb.P.emit()
        return kb
    reg = Region(R0, ptop.lo)
    kb.WBs = [kb.alloc(reg, "WBp%d" % k, [128, 8, 2, 128], BF16) for k in range(2)]
    n = PN // LCH
    SP = kb.alloc(reg, "SPp", [128, 2, 32, n])
    QT = kb.alloc(reg, "QTp", [128, 2, 32, n + 1])
    T = [kb.alloc(reg, "Tp%d" % k, [128, 32, n]) for k in range(4)]
    kb.dma("sp", QT, kb.PQ_d.rearrange("t p j c -> p t j c"), "qtp")
    kb.prefix_s5(0, Upf[0], HN, (SP, QT, T))
    kb.prefix_s5(1, Upf[1], HN, (SP, QT, T))
    if stop_after == "prefix":
        kb.dma("sp", din["o_hp"][0], kb.Hc[:, 0, :], "ohp0")
        kb.dma("sp", din["o_hp"][1], kb.Hc[:, 1, :], "ohp1")
        kb.P.emit()
        return kb
    for ci in range(2):
        kb.emit_chunk(ci, R0, TOP)
    kb.dma("sp", din["o_hp"][0], kb.Hc[:, 0, :], "ohp0")
    kb.dma("sp", din["o_hp"][1], kb.Hc[:, 1, :], "ohp1")
    kb.P.emit()
    return kb


POOL_W = (2, 4, 8, 16)


def make_in_maps(inp):
    f = lambda a: np.ascontiguousarray(a, dtype=np.float32)
    shared = {
        "w_in": f(inp["w_in"][0]), "w_pool": f(inp["w_pool"][0]), "w_glu": f(inp["w_glu"][0]),
        "w_branch_pool": f(inp["w_branch_pool"][0]), "w_branch_ssm": f(inp["w_branch_ssm"][0]),
        "w_out": f(inp["w_out"][0]), "w_up": f(inp["w_up"][0]), "w_down": f(inp["w_down"][0]),
        "w_ple_gate": f(inp["w_ple_gate"][0]), "w_ple": f(inp["w_ple"][0]),
        "ones": np.ones((128, 128), np.float32), "ident": np.eye(128, dtype=np.float32),
        "mask": np.kron(np.eye(8, dtype=np.float32), np.ones((16, 16), np.float32)),
        "b_re": f(inp["ssm_b_re"][0]), "b_im": f(inp["ssm_b_im"][0]),
        "c_re": f(inp["ssm_c_re"][0]), "c_im": f(inp["ssm_c_im"][0]),
    }
    col = lambda v: np.asarray(v, np.float32).reshape(-1, 128).T
    shared["vecs"] = f(np.concatenate([col(inp["g_mix"][0]), col(inp["g_ffn"][0]), col(inp["g_ple"][0]), col(inp["g_final"]),
                                       col(inp["pool_scale"][0]), col(inp["ssm_d"][0]), col(inp["w_conv"][0][0]),
                                       col(inp["w_conv"][0][1]), col(inp["w_conv"][0][2]), col(inp["b_conv"][0])], axis=1))
    sm = lambda a: np.asarray(a, np.float32).reshape(32, 128).T
    ldt = np.repeat(np.asarray(inp["ssm_log_dt"][0], np.float32)[:, None], 64, axis=1)
    shared["lamv"] = f(np.stack([sm(inp["ssm_lam_re"][0]), sm(inp["ssm_lam_im"][0]), sm(ldt)], axis=1))
    xp, xs = inp["x_prompt"], inp["x_sample"][:, 0, :]
    pp, psm = inp["p_prompt"][0], inp["p_sample"][0][:, 0, :]
    maps = []
    for c in range(8):
        b, half = c // 2, c % 2
        m = dict(shared)
        s0 = half * OWN
        first = xp[b, 0:OWN] if half else np.zeros((OWN, DM), np.float32)
        m["xcat"] = f(np.concatenate([first, xp[b, s0:s0 + OWN], xs[NSMP * c:NSMP * c + NSMP]], axis=0))
        pfirst = pp[b, s0 - PRE:s0] if half else np.zeros((PRE, DPLE), np.float32)
        m["pcat"] = f(np.concatenate([pfirst, pp[b, s0:s0 + OWN], psm[NSMP * c:NSMP * c + NSMP]], axis=0))
        m["cpool"] = f(inp["cache_pool"][0, NSMP * c:NSMP * c + NSMP])
        m["st_re"] = f(inp["state_ssm_re"][0, NSMP * c:NSMP * c + NSMP].reshape(NSMP, 4096))
        m["st_im"] = f(inp["state_ssm_im"][0, NSMP * c:NSMP * c + NSMP].reshape(NSMP, 4096))
        m["cconv"] = f(inp["cache_conv"][0, NSMP * c:NSMP * c + NSMP])
        iv = np.zeros((128, 4, 16), np.float32)
        for k, w in enumerate(POOL_W):
            for t in range(16):
                iv[:, k, t] = 1.0 / (w if half else min(w, t + 1))
        m["invc"] = iv
        maps.append(m)
    return maps


_PROG = {}


def kernel(**inputs):
    inp = {k: np.asarray(v) for k, v in inputs.items()}
    if "kb" not in _PROG:
        _PROG["kb"] = build_program()
    kb = _PROG["kb"]
    maps = make_in_maps(inp)
    res = run_bass_kernel_spmd(kb.nc, maps, core_ids=list(range(8)))
    R = res.results
    B = 4
    y_p = np.zeros((B, 2 * OWN, DM), np.float32)
    y_s = np.zeros((128, 1, DM), np.float32)
    pool_p = np.zeros((1, B, 15, DP), np.float32)
    re_p = np.zeros((1, B, 64, 64), np.float32)
    im_p = np.zeros((1, B, 64, 64), np.float32)
    conv_p = np.zeros((1, B, 2, DFF), np.float32)
    pool_s = np.zeros((1, 128, 15, DP), np.float32)
    re_s = np.zeros((1, 128, 64, 64), np.float32)
    im_s = np.zeros((1, 128, 64, 64), np.float32)
    conv_s = np.zeros((1, 128, 2, DFF), np.float32)
    for c in range(8):
        b, half = c // 2, c % 2
        r = R[c]
        y_p[b, half * OWN:(half + 1) * OWN] = r["o_y"][0:OWN]
        y_s[NSMP * c:NSMP * c + NSMP, 0] = r["o_y"][OWN:OWN + NSMP]
        if half:
            pool_p[0, b] = r["o_poolp"]
            re_p[0, b] = r["o_hp"][0].T.reshape(64, 64)
            im_p[0, b] = r["o_hp"][1].T.reshape(64, 64)
            conv_p[0, b] = r["o_convp"]
        pool_s[0, NSMP * c:NSMP * c + NSMP] = r["o_pools"]
        re_s[0, NSMP * c:NSMP * c + NSMP] = r["o_sre"].reshape(NSMP, 64, 64)
        im_s[0, NSMP * c:NSMP * c + NSMP] = r["o_sim"].reshape(NSMP, 64, 64)
        conv_s[0, NSMP * c:NSMP * c + NSMP] = r["o_convs"]
    return (y_p, y_s, pool_p, re_p, im_p, conv_p, pool_s, re_s, im_s, conv_s)
```

```python
import numpy as np
import concourse.bass as bass
import concourse.mybir as mybir
from concourse.bass_utils import run_bass_kernel_spmd

F32 = mybir.dt.float32
F32R = mybir.dt.float32r
BF16 = mybir.dt.bfloat16
ALU = mybir.AluOpType
AF = mybir.ActivationFunctionType


class Op:
    __slots__ = ("eng", "emit", "deps", "idx", "signal", "semval", "dma", "semkey", "name")

    def __init__(self, eng, emit, dma, semkey, name):
        self.eng = eng
        self.emit = emit
        self.deps = []
        self.idx = -1
        self.signal = False
        self.semval = 0
        self.dma = dma
        self.semkey = semkey
        self.name = name


class Prog:
    ENGS = ("pe", "act", "dve", "pool", "sp")

    def __init__(self, nc):
        self.nc = nc
        self.q = {e: [] for e in self.ENGS}
        self.writers = {}
        self.readers = {}
        self.dma_keys = {}
        self.bulk_keys = set()

    def op(self, eng, emit, reads=(), writes=(), dma=False, semkey=None, name=""):
        o = Op(eng, emit, dma, semkey, name)
        deps = []
        for k in reads:
            deps += self.writers.get(k, [])
        for k in writes:
            for w in self.writers.get(k, ()):
                if not (dma and w.dma):
                    deps.append(w)
            deps += self.readers.get(k, [])
        seen = set()
        for d in deps:
            if id(d) in seen or d is o:
                continue
            if dma and d.dma and d.semkey == semkey and semkey in self.bulk_keys:
                continue
            seen.add(id(d))
            o.deps.append(d)
        for k in writes:
            if self.readers.get(k):
                self.writers[k] = [o]
                self.readers[k] = []
            else:
                self.writers.setdefault(k, []).append(o)
                if len(self.writers[k]) > 64:
                    self.writers[k] = self.writers[k][-64:]
        for k in reads:
            if k not in writes:
                self.readers.setdefault(k, []).append(o)
        o.idx = len(self.q[eng])
        self.q[eng].append(o)
        if dma:
            assert semkey is not None
            self.dma_keys.setdefault(semkey, []).append(o)
        return o

    def _needs_sync(self, c, p):
        if p.dma:
            return True
        if p.eng != c.eng:
            return True
        if c.dma:
            return True
        if p.eng == "pe":
            return False
        return True

    def emit(self):
        nc = self.nc
        for e in self.ENGS:
            for c in self.q[e]:
                for p in c.deps:
                    if (not p.dma) and self._needs_sync(c, p):
                        p.signal = True
            last = [o for o in self.q[e] if not o.dma]
            if last:
                last[-1].signal = True
        import contextlib

        stack = contextlib.ExitStack()
        sems = {}
        for e in ("pe", "act", "dve", "pool"):
            sems[e] = stack.enter_context(nc.semaphore("sem_" + e))
        dsems = {}
        for k in self.dma_keys:
            dsems[k] = stack.enter_context(nc.semaphore("dsem_%s" % (str(k).replace(" ", ""))))
        for e in ("pe", "act", "dve", "pool", "sp"):
            cnt = 0
            for o in self.q[e]:
                if o.dma:
                    continue
                if o.signal:
                    cnt += 1
                    o.semval = cnt
        for k, ops in self.dma_keys.items():
            cnt = 0
            for o in ops:
                cnt += 16
                o.semval = cnt
            if k in self.bulk_keys:
                for o in ops:
                    o.semval = cnt
        self.n_signals = {e: sum(1 for o in self.q[e] if o.signal) for e in self.ENGS}

        def run_queue(ename, eng):
            waited = {}
            for o in self.q[ename]:
                need = {}
                for p in o.deps:
                    if p.dma:
                        s = dsems[p.semkey]
                        v = p.semval
                    else:
                        if not p.signal:
                            continue
                        if not self._needs_sync(o, p):
                            continue
                        s = sems[p.eng]
                        v = p.semval
                    key = id(s)
                    if need.get(key, (None, 0))[1] < v:
                        need[key] = (s, v)
                for key, (s, v) in need.items():
                    if waited.get(key, 0) >= v:
                        continue
                    eng.wait_ge(s, v)
                    waited[key] = v
                inst = o.emit(eng)
                if o.dma:
                    inst.then_inc(dsems[o.semkey], 16)
                elif o.signal:
                    inst.then_inc(sems[o.eng], 1)
            return waited

        final_waits = []
        with nc.Block() as block:

            @block.sync
            def _(e):
                w = run_queue("sp", e)
                for k, ops in self.dma_keys.items():
                    e.wait_ge(dsems[k], 16 * len(ops))
                for en in ("pe", "act", "dve", "pool"):
                    if self.n_signals[en]:
                        e.wait_ge(sems[en], self.n_signals[en])

            @block.scalar
            def _(e):
                run_queue("act", e)

            @block.vector
            def _(e):
                run_queue("dve", e)

            @block.gpsimd
            def _(e):
                run_queue("pool", e)

            @block.tensor
            def _(e):
                run_queue("pe", e)

        stack.close()


DTSIZE = {F32: 4, F32R: 4, BF16: 2, mybir.dt.int32: 4}
I32 = mybir.dt.int32


class Region:
    def __init__(self, lo, hi):
        self.lo, self.hi, self.cur = lo, hi, lo

    def take(self, nbytes):
        nbytes = (nbytes + 31) // 32 * 32
        self.cur = (self.cur + 31) // 32 * 32
        off = self.cur
        self.cur += nbytes
        assert self.cur <= self.hi, ("region overflow", self.lo, self.hi, self.cur)
        return off


class KB:
    def __init__(self):
        self.nc = bass.Bass("TRN2", target_bir_lowering=False, dynamic_dma_scratch_size=256)
        self.nc.dge_precook = False
        self.P = Prog(self.nc)
        self.ps = [self.nc.alloc_psum_tensor("ps%d" % i, [128, 512], F32).ap() for i in range(8)]
        self.allocs = {}
        self.kcache = {}
        self.uid = 0
        self.dbg_outs = []

    def alloc(self, region, name, shape, dt=F32):
        n = 1
        for d in shape[1:]:
            n *= d
        nbytes = n * DTSIZE[dt]
        off = region.take(nbytes)
        self.uid += 1
        t = self.nc.alloc_sbuf_tensor_at("%s_%d" % (name, self.uid), list(shape), dt, offset=off)
        ap = t.ap()
        self.allocs[ap.name] = (off, off + (nbytes + 31) // 32 * 32)
        return ap

    KEYB = 32

    def keys(self, ap):
        nm = ap.name
        r = self.kcache.get(nm)
        if r is not None:
            return r
        if nm not in self.allocs:
            r = [nm]
        else:
            lo, hi = self.allocs[nm]
            r = [("b", k) for k in range(lo // self.KEYB, (hi - 1) // self.KEYB + 1)]
        self.kcache[nm] = r
        return r

    def _op(self, eng, fn, outs, ins, **kw):
        rd, wr = [], []
        for a in ins:
            rd += self.keys(a)
            if a.name.startswith("ps"):
                wr += self.keys(a)
        for a in outs:
            wr += self.keys(a)
        return self.P.op(eng, fn, reads=rd, writes=wr, **kw)

    def mm(self, out, lhsT, rhs, start=True, stop=True, tp=None):
        self._op("pe", lambda e: e.matmul(out, lhsT=lhsT, rhs=rhs, start=start, stop=stop, tile_position=tp),
                 [out], [lhsT, rhs])

    def tr(self, out, in_, ident):
        self._op("pe", lambda e: e.transpose(out, in_, ident), [out], [in_, ident])

    def act(self, out, in_, func, scale=1.0, bias=0.0):
        ins = [in_]
        if not isinstance(scale, (int, float)):
            ins.append(scale)
        if not isinstance(bias, (int, float)):
            ins.append(bias)
        self._op("act", lambda e: e.activation(out, in_, func, bias=bias, scale=scale), [out], ins)

    def _e(self, eng):
        return eng

    def tt(self, eng, out, a, b, op):
        self._op(eng, lambda e: e.tensor_tensor(out, a, b, op), [out], [a, b])

    def ts(self, eng, out, a, s1, op0, s2=None, op1=None):
        ins = [a] + [s for s in (s1, s2) if s is not None and not isinstance(s, (int, float))]
        if op1 is None:
            self._op(eng, lambda e: e.tensor_scalar(out, a, s1, None, op0), [out], ins)
        else:
            self._op(eng, lambda e: e.tensor_scalar(out, a, s1, s2, op0, op1), [out], ins)

    def stt(self, out, in0, scalar, in1, op0, op1):
        ins = [in0, in1] + ([] if isinstance(scalar, (int, float)) else [scalar])
        self._op("dve", lambda e: e.scalar_tensor_tensor(out, in0, scalar, in1, op0, op1), [out], ins)

    def cp(self, eng, out, in_):
        if eng == "act":
            self._op("act", lambda e: e.copy(out, in_), [out], [in_])
        else:
            self._op(eng, lambda e: e.tensor_copy(out, in_), [out], [in_])

    def ms(self, eng, ap, val):
        self._op(eng, lambda e: e.memset(ap, val), [ap], [])

    def scan(self, out, d0, d1, init=0.0):
        self._op("dve", lambda e: e.tensor_tensor_scan(out, d0, d1, init, ALU.mult, ALU.add), [out], [d0, d1])

    def recip(self, out, in_):
        self._op("dve", lambda e: e.reciprocal(out, in_), [out], [in_])

    def red(self, out, in_):
        self._op("dve", lambda e: e.tensor_reduce(out, in_, mybir.AxisListType.X, ALU.add), [out], [in_])

    def dma(self, q, out, in_, key, slow=False):
        self._op(q, lambda e: e.dma_start(out=out, in_=in_, allow_slow_non_contiguous=slow), [out], [in_],
                 dma=True, semkey=key)

    def dbg(self, name, ap, region):
        shp = list(ap.shape)
        d = self.nc.dram_tensor("dbg_" + name, shp, ap.dtype, kind="ExternalOutput").ap()
        self.dma("sp", d, ap, key="dbg_" + name)
        self.dbg_outs.append("dbg_" + name)


DM = 2048
KT = 16
DP = 1024
NPT = 8
DFF = 6144
NFT = 48
DPLE = 256
DIN = 6144
PRE = 32
OWN = 1024
NSMP = 16
NC = 536
NT = [(0, 272), (272, 536)]
PFX = 992
PN = 496
NCHM = 67
LCH = 8
EPS = 1e-6
SEQS = [536, 520]
TWO_PI = 6.283185307179586
CW1 = 6.28125
CW2 = float(np.float32(TWO_PI - CW1))
CW3 = float(TWO_PI - CW1 - CW2)


def bc(ap, shape):
    return ap.to_broadcast(list(shape))


class Step(KB):
    def poly(self, out, x, coefs):
        n = len(coefs) - 1
        self.ts("dve", out, x, float(coefs[n]), ALU.mult)
        for k in range(n - 1, 0, -1):
            self.stt(out, out, float(coefs[k]), x, ALU.add, ALU.mult)
        self.ts("dve", out, out, float(coefs[0]), ALU.add)

    def cmul(self, eng, o_re, o_im, a_re, a_im, b_re, b_im, t1, t2, t3=None, t4=None):
        e1, e2 = (eng, eng) if eng != "split" else ("dve", "pool")
        if eng != "split":
            t3, t4 = t1, t2
        self.tt(e1, t1, a_re, b_re, ALU.mult)
        self.tt(e1, t2, a_im, b_im, ALU.mult)
        if eng == "split":
            self.tt(e2, t3, a_re, b_im, ALU.mult)
            self.tt(e2, t4, a_im, b_re, ALU.mult)
            self.tt(e1, o_re, t1, t2, ALU.subtract)
            self.tt(e2, o_im, t3, t4, ALU.add)
        else:
            self.tt(e1, o_re, t1, t2, ALU.subtract)
            self.tt(e1, t1, a_re, b_im, ALU.mult)
            self.tt(e1, t2, a_im, b_re, ALU.mult)
            self.tt(e1, o_im, t1, t2, ALU.add)

    def emit_setup(self, din, pers, reg, dbg=False):
        A = lambda name, shape, dt=F32: self.alloc(reg, name, shape, dt)
        PA = lambda name, shape, dt=F32: self.alloc(pers, name, shape, dt)
        nc = self.nc
        self.P.bulk_keys.add("su")
        LR, LI, LDT = A("LR", [128, 32]), A("LI", [128, 32]), A("LDT", [128, 32])
        self.dma("sp", LR, din["lamv"][:, 0, :], "su")
        self.dma("sp", LI, din["lamv"][:, 1, :], "su")
        self.dma("sp", LDT, din["lamv"][:, 2, :], "su")
        mask = A("mask", [128, 128])
        if getattr(self, "ident", None) is None:
            self.ident = PA("ident", [128, 128])
            self.dma("sp", self.ident, din["ident"], "su")
        ident = self.ident
        self.dma("sp", mask, din["mask"], "su")
        dsk = self.dsk
        Bz = [A("Bz%d" % r, [128, 32, 32]) for r in range(2)]
        for r, nm in enumerate(("b_re", "b_im")):
            self.ms("dve", Bz[r], 0.0)
            bv = din[nm].rearrange("g p h -> (g p) h").rearrange("(j q) h -> q j h", q=128)
            for g2 in range(2):
                self.dma("sp", Bz[r][64 * g2:64 * g2 + 64, :, 16 * g2:16 * g2 + 16], bv[64 * g2:64 * g2 + 64], "su")
        Cz = [A("Cz%d" % r, [128, 32, 32]) for r in range(2)]
        C2 = [A("C2_%d" % k, [128, 128]) for k in range(2)]
        n_tr = 0
        for r, nm in enumerate(("c_re", "c_im")):
            self.ms("dve", Cz[r], 0.0)
            cv = din[nm].rearrange("g h p -> (g h) p")
            for i in range(8):
                c2 = C2[n_tr % 2]
                for hh in range(2):
                    self.dma("sp", c2[:, 64 * hh:64 * hh + 64], cv[128 * i:128 * i + 128, :], "suc%d_%d" % (n_tr % 2, hh))
                pst = self.ps[n_tr % 2]
                self.tr(pst[:, 0:128], c2, ident)
                for g2 in range(2):
                    src = pst[64 * g2:64 * g2 + 64, 0:128].rearrange("p (q a h) -> p q a h", q=4, a=2)[:, :, g2, :]
                    self.cp("dve", Cz[r][64 * g2:64 * g2 + 64, 4 * i:4 * i + 4, 16 * g2:16 * g2 + 16], src)
                n_tr += 1
        T = lambda nm: A(nm, [128, 32])
        dt_, X, TH, mag = T("dt"), T("X"), T("TH"), T("mag")
        fact = [1.0]
        for k in range(1, 20):
            fact.append(fact[-1] * k)
        self.ts("dve", LDT, LDT, 0.125, ALU.mult)
        self.poly(dt_, LDT, [1.0 / fact[k] for k in range(14)])
        for _ in range(3):
            self.tt("dve", dt_, dt_, dt_, ALU.mult)
        self.tt("dve", X, LR, dt_, ALU.mult)
        self.tt("dve", TH, LI, dt_, ALU.mult)
        self.poly(mag, X, [1.0 / fact[k] for k in range(8)])
        kf, ki, r4, z, sn, cs, t1, t2 = T("kf"), A("ki", [128, 32], I32), T("r4"), T("z"), T("sn"), T("cs"), T("t1"), T("t2")
        self.ts("dve", kf, TH, 1.0 / TWO_PI, ALU.mult)
        self.cp("dve", ki, kf)
        self.cp("dve", kf, ki)
        self.stt(r4, kf, -CW1, TH, ALU.mult, ALU.add)
        self.stt(r4, kf, -CW2, r4, ALU.mult, ALU.add)
        self.stt(r4, kf, -CW3, r4, ALU.mult, ALU.add)
        self.ts("dve", r4, r4, 0.25, ALU.mult)
        self.tt("dve", z, r4, r4, ALU.mult)
        self.poly(sn, z, [((-1.0) ** k) / fact[2 * k + 1] for k in range(8)])
        self.tt("dve", sn, sn, r4, ALU.mult)
        self.poly(cs, z, [((-1.0) ** k) / fact[2 * k] for k in range(9)])
        for _ in range(2):
            self.stt(t1, sn, 2.0, cs, ALU.mult, ALU.mult)
            self.tt("dve", t2, sn, sn, ALU.mult)
            self.ts("dve", cs, t2, -2.0, ALU.mult, 1.0, ALU.add)
            self.cp("dve", sn, t1)
        ar, ai = PA("ar", [128, 32]), PA("ai", [128, 32])
        self.tt("dve", ar, mag, cs, ALU.mult)
        self.tt("dve", ai, mag, sn, ALU.mult)
        self.ar1, self.ai1 = ar, ai
        nr, den, cr, ci = T("nr"), T("den"), T("cr"), T("ci")
        self.ts("dve", nr, ar, -1.0, ALU.add)
        self.tt("dve", t1, LR, LR, ALU.mult)
        self.tt("dve", t2, LI, LI, ALU.mult)
        self.tt("dve", den, t1, t2, ALU.add)
        self.recip(den, den)
        self.tt("dve", t1, nr, LR, ALU.mult)
        self.tt("dve", t2, ai, LI, ALU.mult)
        self.tt("dve", cr, t1, t2, ALU.add)
        self.tt("dve", cr, cr, den, ALU.mult)
        self.tt("dve", t1, ai, LR, ALU.mult)
        self.tt("dve", t2, nr, LI, ALU.mult)
        self.tt("dve", ci, t1, t2, ALU.subtract)
        self.tt("dve", ci, ci, den, ALU.mult)
        PWr, PWi = A("PWr", [128, 9, 32]), A("PWi", [128, 9, 32])
        self.ms("dve", PWr[:, 0, :], 1.0)
        self.ms("dve", PWi[:, 0, :], 0.0)
        self.cp("dve", PWr[:, 1, :], ar)
        self.cp("dve", PWi[:, 1, :], ai)
        for p in range(2, 9):
            self.cmul("dve", PWr[:, p, :], PWi[:, p, :], PWr[:, p - 1, :], PWi[:, p - 1, :], ar, ai, t1, t2)
        r8, u8c, u8s = PA("r8", [128, 32]), PA("u8c", [128, 32]), PA("u8s", [128, 32])
        self.tt("dve", r8, mag, mag, ALU.mult)
        self.tt("dve", r8, r8, r8, ALU.mult)
        self.tt("dve", r8, r8, r8, ALU.mult)
        wc, ws = T("wc"), T("ws")
        self.cp("dve", wc, cs)
        self.cp("dve", ws, sn)
        for _ in range(3):
            self.cmul("dve", z, kf, wc, ws, wc, ws, t1, t2)
            self.cp("dve", wc, z)
            self.cp("dve", ws, kf)
        self.cp("dve", u8c, wc)
        self.cp("dve", u8s, ws)
        self.r8, self.u8c, self.u8s = r8, u8c, u8s
        ETc, ETs = A("ETc", [128, 32, NCHM]), A("ETs", [128, 32, NCHM])
        X1, X2 = A("X1", [128, 32, 32]), A("X2", [128, 32, 32])
        BB = [A("BB%d" % r, [128, 32, 32]) for r in range(2)]
        crb, cib = bc(cr.unsqueeze(2), [128, 32, 32]), bc(ci.unsqueeze(2), [128, 32, 32])
        self.cmul("dve", BB[0], BB[1], Bz[0], Bz[1], crb, cib, X1, X2)
        X3 = self.alloc(Region(*self.allocs[Bz[0].name]), "X3", [128, 32, 32])
        X4 = self.alloc(Region(*self.allocs[Bz[1].name]), "X4", [128, 32, 32])
        self.ms("dve", ETc[:, :, 0:1], 1.0)
        self.ms("dve", ETs[:, :, 0:1], 0.0)
        s = 1
        while s < NCHM:
            m = min(s, NCHM - s)
            wcb, wsb = bc(wc.unsqueeze(2), [128, 32, m]), bc(ws.unsqueeze(2), [128, 32, m])
            self.cmul("dve", ETc[:, :, s:s + m], ETs[:, :, s:s + m], ETc[:, :, 0:m], ETs[:, :, 0:m], wcb, wsb,
                      X1[:, :, 0:m], X2[:, :, 0:m], X3[:, :, 0:m], X4[:, :, 0:m])
            s *= 2
            if s < NCHM:
                self.cmul("dve", z, kf, wc, ws, wc, ws, t1, t2)
                self.cp("dve", wc, z)
                self.cp("dve", ws, kf)
        yield "etables"
        TAB_d = nc.dram_tensor("TAB_d", [2, 128, 32, NCHM], F32, kind="Internal").ap()
        self.P.bulk_keys.add("tab")
        for t, src in enumerate((ETc, ETs)):
            self.dma("sp", TAB_d[t], src, "tab")
        self.TAB_d = TAB_d
        QL = PN // LCH + 1
        Qr = self.alloc(Region(*self.allocs[ETc.name]), "Qr", [128, 32, QL])
        Qi = self.alloc(Region(*self.allocs[ETs.name]), "Qi", [128, 32, QL])
        self.ms("dve", Qr[:, :, QL - 1:QL], 1.0)
        self.ms("dve", Qi[:, :, QL - 1:QL], 0.0)
        self.cp("dve", wc, PWr[:, 8, :])
        self.cp("dve", ws, PWi[:, 8, :])
        s = 1
        while s < QL:
            m = min(s, QL - s)
            wcb, wsb = bc(wc.unsqueeze(2), [128, 32, m]), bc(ws.unsqueeze(2), [128, 32, m])
            self.cmul("dve", Qr[:, :, QL - s - m:QL - s], Qi[:, :, QL - s - m:QL - s], Qr[:, :, QL - m:QL], Qi[:, :, QL - m:QL],
                      wcb, wsb, X1[:, :, 0:m], X2[:, :, 0:m], X3[:, :, 0:m], X4[:, :, 0:m])
            s *= 2
            if s < QL:
                self.cmul("dve", z, kf, wc, ws, wc, ws, t1, t2)
                self.cp("dve", wc, z)
                self.cp("dve", ws, kf)
        PQ_d = self.nc.dram_tensor("PQ_d", [2, 128, 32, QL], F32, kind="Internal").ap()
        self.P.bulk_keys.add("tabq")
        self.dma("sp", PQ_d[0], Qr, "tabq")
        self.dma("sp", PQ_d[1], Qi, "tabq")
        self.PQ_d, self.QL = PQ_d, QL
        yield "tables"
        WB_d = nc.dram_tensor("WB_d", [128, 8, 8, 2, 128], BF16, kind="Internal").ap()
        WC_d = nc.dram_tensor("WC_d", [128, 32, 8, 2, 32], BF16, kind="Internal").ap()
        KW_d = nc.dram_tensor("KW_d", [128, 8, 8, 128], BF16, kind="Internal").ap()
        self.WB_d, self.WC_d, self.KW_d = WB_d, WC_d, KW_d
        NCz = A("NCz", [128, 32, 32])
        self.ts("dve", NCz, Cz[1], -1.0, ALU.mult)
        rv = [Region(*self.allocs[ETc.name]), Region(*self.allocs[ETs.name])]
        V = [[self.alloc(rv[r], "V%d_%d" % (k, r), [128, 32, 32]) for r in range(2)] for k in range(2)]
        WBS = [A("WBS0", [128, 8, 2, 128], BF16)] * 2
        KWS = [A("KWS0", [128, 8, 128], BF16)] * 2
        WCS = [A("WCS0", [128, 32, 2, 32], BF16)] * 2
        tmpK = A("tmpK", [128, 128])
        def wc_part(tau):
            prb = bc(PWr[:, tau, :].unsqueeze(2), [128, 32, 32])
            pib = bc(PWi[:, tau, :].unsqueeze(2), [128, 32, 32])
            l = tau - 1
            wcs = WCS[tau % 2]
            self.tt("dve", X1, Cz[0], prb, ALU.mult)
            self.tt("dve", X2, Cz[1], pib, ALU.mult)
            self.tt("dve", X3, NCz, prb, ALU.mult)
            self.tt("dve", X4, Cz[0], pib, ALU.mult)
            self.tt("dve", wcs[:, :, 0, :], X1, X2, ALU.subtract)
            self.tt("dve", wcs[:, :, 1, :], X3, X4, ALU.subtract)
            self.dma("sp", WC_d[:, :, l, :, :], wcs, "wcs0")

        for tau in range(9):
            prb = bc(PWr[:, tau, :].unsqueeze(2), [128, 32, 32])
            pib = bc(PWi[:, tau, :].unsqueeze(2), [128, 32, 32])
            if tau <= 7:
                Vr, Vi = V[tau % 2]
                self.cmul("dve", Vr, Vi, BB[0], BB[1], prb, pib, X1, X2)
                l = 7 - tau
                wbs, kws = WBS[tau % 2], KWS[tau % 2]
                Vv = [Vr.rearrange("p j c -> p (j c)").rearrange("p (i c) -> p i c", i=8),
                      Vi.rearrange("p j c -> p (j c)").rearrange("p (i c) -> p i c", i=8)]
                Czv = [Cz[0].rearrange("p j c -> p (j c)").rearrange("p (i c) -> p i c", i=8),
                       NCz.rearrange("p j c -> p (j c)").rearrange("p (i c) -> p i c", i=8)]
                for i in range(8):
                    pst = self.ps[2 + (i % 2)]
                    for r in range(2):
                        self.tr(pst[:, 128 * r:128 * r + 128], Vv[r][:, i, :], ident)
                    self.cp("act", wbs[:, i, :, :], pst[:, 0:256].rearrange("p (r c) -> p r c", r=2))
                    psk = self.ps[4 + (i % 4)]
                    self.mm(psk[:, 0:128], lhsT=Vv[0][:, i, :], rhs=Czv[0][:, i, :], start=True, stop=False)
                    self.mm(psk[:, 0:128], lhsT=Vv[1][:, i, :], rhs=Czv[1][:, i, :], start=False, stop=True)
                    if i == 3 and tau >= 1:
                        wc_part(tau)
                    if tau == 0:
                        self.tt("dve", tmpK, psk[:, 0:128], mask, ALU.mult)
                        self.stt(kws[:, i, :], ident, dsk[:, i:i + 1], tmpK, ALU.mult, ALU.add)
                    else:
                        self.tt("dve", kws[:, i, :], psk[:, 0:128], mask, ALU.mult)
                self.dma("sp", WB_d[:, :, l, :, :], wbs, "wbs0")
                self.dma("sp", KW_d[:, :, tau, :], kws, "kws0")
            else:
                wc_part(tau)
            if tau == 3:
                yield "half"
        if dbg:
            for nm, ap in (("ar", ar), ("ai", ai), ("cr", cr), ("ci", ci), ("r8", r8), ("u8c", u8c), ("u8s", u8s),
                           ("ETc", ETc), ("ETs", ETs)):
                self.dbg(nm, ap, reg)

    def init_main(self, din, pers):
        self.din = din
        PA = lambda name, shape, dt=F32: self.alloc(pers, name, shape, dt)
        self.xT = [PA("xT%d" % k, [128, NC]) for k in range(KT)]
        self.hT = [PA("hT%d" % k, [128, NC], F32R) for k in range(KT)]
        self.NW = 4
        self.wslot = [PA("W%d" % k, [128, 2048], F32R) for k in range(self.NW)]
        self.wi = 0
        self.ident = PA("ident", [128, 128])
        self.dma("sp", self.ident, din["ident"], "su0")
        self.ones = PA("ones", [128, 128], F32R)
        self.dma("sp", self.ones, din["ones"], "su0")
        self.gcol = {}
        vecs = PA("vecs", [128, 272])
        self.dma("sp", vecs, din["vecs"], "su0")
        for k, nm in enumerate(("g_mix", "g_ffn", "g_ple", "g_final")):
            self.gcol[nm] = vecs[:, 16 * k:16 * k + 16]
        self.pscale = vecs[:, 64:72]
        self.dsk = vecs[:, 72:80]
        self.wconv = vecs[:, 80:224].rearrange("p (j t) -> p j t", j=3)
        self.bconv = vecs[:, 224:272]
        self.invc = PA("invc", [128, 4, 16])
        self.dma("sp", self.invc, din["invc"], "su0")
        self.sq = [PA("sq%d" % k, [128, NC], F32R) for k in range(2)]
        self.rstd = PA("rstd", [128, NC])
        self.sig = [PA("sig%d" % k, [128, 272]) for k in range(4)]
        self.sgi = 0
        self.TAILP = PA("TAILP", [128, NPT, 16])
        self.ms("dve", self.TAILP, 0.0)
        self.CT = PA("CT", [128, NFT, 2])
        self.ms("dve", self.CT, 0.0)
        self.Hc = PA("Hc", [128, 2, 32])
        self.ms("dve", self.Hc, 0.0)
        self.bank = 0
        self.xti = 0
        self.xcopy_act = False

    def next_sig(self):
        t = self.sig[self.sgi % 4]
        self.sgi += 1
        return t

    def wload(self, view, kt, mc):
        k = self.wi % self.NW
        self.wi += 1
        ap = self.wslot[k][:, 0:kt * mc].rearrange("p (k c) -> p k c", k=kt)
        self.dma("sp", ap, view, "w%d" % k)
        return ap

    def wview(self, name, c0, mc, k0=0, kt=None):
        w = self.din[name]
        v = w.rearrange("(k p) c -> p k c", p=128)
        if kt is None:
            kt = v.shape[1] - k0
        return v[:, k0:k0 + kt, c0:c0 + mc]

    def mm_acc(self, psb, wblk, cs, rhs, kt, nts):
        for n, (a, b) in enumerate(nts):
            for k in range(kt):
                self.mm(self.ps[psb[n]][:, 0:b - a], lhsT=wblk[:, k, cs:cs + 128], rhs=rhs[k][:, a:b],
                        start=(k == 0), stop=(k == kt - 1))

    def load_xT(self, xrows, ncols, col0=0, dst=None, stg=None):
        t0 = 0
        dst = dst or self.xT
        stg = stg or self.xtok
        while t0 < ncols:
            nt = min(128, ncols - t0)
            xt = stg[self.xti % 2]
            self.xti += 1
            self.dma("act", xt[0:nt, :], xrows[t0:t0 + nt, :], "xin%d" % (self.xti % 2))
            for kg in range(4):
                pst = self.ps[self.bank % 8]
                self.bank += 1
                for kk in range(4):
                    k = 4 * kg + kk
                    self.tr(pst[:, 128 * kk:128 * kk + nt], xt[0:nt, 128 * k:128 * k + 128], self.ident[0:nt, 0:nt])
                for kk in range(4):
                    k = 4 * kg + kk
                    eng = "act" if (kk % 2 == 0 or self.xcopy_act) else "dve"
                    self.cp(eng, dst[k][:, col0 + t0:col0 + t0 + nt], pst[:, 128 * kk:128 * kk + nt])
            t0 += nt

    def rmsnorm(self, gname, nts, ncols, out=None):
        out = out or self.hT
        g = self.gcol[gname]
        b0 = self.bank
        self.bank += 2
        for k in range(KT):
            sq = self.sq[k % 2]
            self.act(sq[:, 0:ncols], self.xT[k][:, 0:ncols], AF.Square)
            for n, (a, b) in enumerate(nts):
                self.mm(self.ps[(b0 + n) % 8][:, 0:b - a], lhsT=self.ones, rhs=sq[:, a:b], start=(k == 0), stop=(k == KT - 1))
        for n, (a, b) in enumerate(nts):
            self.act(self.rstd[:, a:b], self.ps[(b0 + n) % 8][:, 0:b - a], AF.Sqrt, scale=1.0 / DM, bias=EPS)
        self.recip(self.rstd[:, 0:ncols], self.rstd[:, 0:ncols])
        for k in range(KT):
            self.stt(out[k][:, 0:ncols], self.xT[k][:, 0:ncols], g[:, k:k + 1], self.rstd[:, 0:ncols], ALU.mult, ALU.mult)

    def s5_alloc(self, reg):
        A = lambda name, shape, dt=F32: self.alloc(reg, name, shape, dt)
        self.WBs = [A("WBs%d" % k, [128, 8, 2, 128], BF16) for k in range(2)]
        self.WCs = [A("WCs%d" % k, [128, 4, 8, 2, 32], BF16) for k in range(2)]
        self.KWs = [A("KWs%d" % k, [128, 8, 128], BF16) for k in range(2)]
        self.TBs = [A("TBs%d" % k, [128, 3, 4, NCHM]) for k in range(2)]
        self.Ss = [A("Ss%d" % k, [128, 2, 4, NCHM]) for k in range(2)]
        self.Hbs = [A("Hbs%d" % k, [128, 2, 4, NCHM], BF16) for k in range(2)]
        self.Mt = [A("Mt%d" % k, [128, 4, NCHM]) for k in range(6)]
        self.INI = A("INI", [128, 2, 32])
        self.tmp32b = [A("tmp32b_%d" % k, [128, 32]) for k in range(2)]
        self.tmp4 = [A("tmp4_%d" % k, [128, 4]) for k in range(2)]

    def s5_fetch_wb(self, i, slot):
        self.dma("sp", self.WBs[slot], self.WB_d[:, i], "wb%d" % slot)

    def s5_fetch_y(self, i, slot):
        self.dma("sp", self.TBs[slot][:, 0:2], self.TAB_d[:, :, 4 * i:4 * i + 4, :].rearrange("t p j c -> p t j c"), "tb%d" % slot)
        self.dma("sp", self.WCs[slot], self.WC_d[:, 4 * i:4 * i + 4], "wc%d" % slot)
        self.dma("sp", self.KWs[slot], self.KW_d[:, i], "kw%d" % slot)

    def s5_mm(self, i, U_i, n, slot):
        wb, S = self.WBs[slot], self.Ss8[i]
        for r in range(2):
            for l in range(LCH):
                for q in range(4):
                    self.mm(self.ps[4 + q][:, r * NCHM:r * NCHM + n], lhsT=wb[32 * q:32 * q + 32, l, r, :],
                            rhs=U_i[32 * q:32 * q + 32, l * n:(l + 1) * n], start=(l == 0), stop=(l == LCH - 1),
                            tp=(32 * q, 0))
        for q in range(4):
            self.cp("act", S[:, :, q, 0:n], self.ps[4 + q][:, 0:2 * NCHM].rearrange("p (r c) -> p r c", r=2)[:, :, 0:n])

    def s5_chain(self, i, n, slot):
        tb, S = self.TBs[slot], self.Ss8[i]
        Ec, Es = tb[:, 0, :, 0:n], tb[:, 1, :, 0:n]
        Sr, Si = S[:, 0, :, 0:n], S[:, 1, :, 0:n]
        M = [m[:, :, 0:n] for m in self.Mt]
        hc_r, hc_i = self.Hc[:, 0, 4 * i:4 * i + 4], self.Hc[:, 1, 4 * i:4 * i + 4]
        ini = self.INI[:, :, 4 * i:4 * i + 4]
        self.tt("dve", M[0], Ec, Sr, ALU.mult)
        self.tt("dve", M[1], Es, Si, ALU.mult)
        self.tt("dve", M[0], M[0], M[1], ALU.add)
        self.tt("dve", M[2], Ec, Si, ALU.mult)
        self.tt("dve", M[3], Es, Sr, ALU.mult)
        self.tt("dve", M[2], M[2], M[3], ALU.subtract)
        for q in range(4):
            r8b = self.r8[:, 4 * i + q:4 * i + q + 1].to_broadcast([128, n])
            self.scan(M[4][:, q, :], r8b, M[0][:, q, :], ini[:, 0, q:q + 1])
            self.scan(M[5][:, q, :], r8b, M[2][:, q, :], ini[:, 1, q:q + 1])
        Hb = self.Hbs[slot]
        self.cp("dve", Hb[:, 0, :, 0:1], hc_r.unsqueeze(2))
        self.cp("dve", Hb[:, 1, :, 0:1], hc_i.unsqueeze(2))
        self.tt("dve", M[0], Ec, M[4], ALU.mult)
        self.tt("dve", M[1], Es, M[5], ALU.mult)
        self.tt("dve", M[2], Ec, M[5], ALU.mult)
        self.tt("dve", M[3], Es, M[4], ALU.mult)
        self.tt("dve", Hb[:, 0, :, 1:n], M[0][:, :, 0:n - 1], M[1][:, :, 0:n - 1], ALU.subtract)
        self.tt("dve", Hb[:, 1, :, 1:n], M[2][:, :, 0:n - 1], M[3][:, :, 0:n - 1], ALU.add)
        self.tt("dve", hc_r, M[0][:, :, n - 1], M[1][:, :, n - 1], ALU.subtract)
        self.tt("dve", hc_i, M[2][:, :, n - 1], M[3][:, :, n - 1], ALU.add)

    def s5_prefix_fetch(self, i, slot):
        tq = self.TBs[slot]
        tqv = tq.rearrange("p t j c -> p (t j c)")[:, 0:2 * 4 * self.QL].rearrange("p (t j c) -> p t j c", t=2, j=4)
        self.dma("sp", self.WBs[slot], self.WB_d[:, i], "wb%d" % slot)
        self.dma("sp", tqv, self.PQ_d[:, :, 4 * i:4 * i + 4, :].rearrange("t p j c -> p t j c"), "tb%d" % slot)

    def s5_prefix(self, i, U_i, n, slot, HN):
        wb, tq, S = self.WBs[slot], self.TBs[slot], self.Ss[slot]
        tqv = tq.rearrange("p t j c -> p (t j c)")[:, 0:2 * 4 * self.QL].rearrange("p (t j c) -> p t j c", t=2, j=4)
        for r in range(2):
            for l in range(LCH):
                for q in range(4):
                    self.mm(self.ps[4 + q][:, r * NCHM:r * NCHM + n], lhsT=wb[32 * q:32 * q + 32, l, r, :],
                            rhs=U_i[32 * q:32 * q + 32, l * n:(l + 1) * n], start=(l == 0), stop=(l == LCH - 1),
                            tp=(32 * q, 0))
        for q in range(4):
            self.cp("act", S[:, :, q, 0:n], self.ps[4 + q][:, 0:2 * NCHM].rearrange("p (r c) -> p r c", r=2)[:, :, 0:n])
        Qr, Qi = tqv[:, 0, :, 1:n + 1], tqv[:, 1, :, 1:n + 1]
        Sr, Si = S[:, 0, :, 0:n], S[:, 1, :, 0:n]
        M = [m[:, :, 0:n] for m in self.Mt]
        self.tt("dve", M[0], Qr, Sr, ALU.mult)
        self.tt("dve", M[1], Qi, Si, ALU.mult)
        self.tt("dve", M[2], Qr, Si, ALU.mult)
        self.tt("dve", M[3], Qi, Sr, ALU.mult)
        self.tt("dve", M[0], M[0], M[1], ALU.subtract)
        self.tt("dve", M[2], M[2], M[3], ALU.add)
        self.red(HN[:, 0, 4 * i:4 * i + 4], M[0])
        self.red(HN[:, 1, 4 * i:4 * i + 4], M[2])

    def s5_y(self, i, U_i, n, slot, GY_i, seq, smp=None):
        wc, kw, Hb = self.WCs[slot], self.KWs[slot], self.Hbs[slot]
        first = [True, True]
        for l in range(LCH):
            hf, lo = l // 4, (l % 4) * n
            for lp in range(l + 1):
                self.mm(self.ps[hf][:, lo:lo + n], lhsT=kw[:, l - lp, :], rhs=U_i[:, lp * n:(lp + 1) * n],
                        start=first[hf], stop=False)
                first[hf] = False
            for q in range(4):
                for r in range(2):
                    last = (l % 4 == 3) and r == 1
                    self.mm(self.ps[hf][32 * q:32 * q + 32, lo:lo + n], lhsT=wc[:, q, l, r, :], rhs=Hb[:, r, q, 0:n],
                            start=False, stop=last, tp=(0, 32 * q))
        for hf in range(2):
            o = GY_i[:, 0:seq].rearrange("p (c l) -> p l c", l=LCH)[:, 4 * hf:4 * hf + 4, :]
            self.act(o, self.ps[hf][:, 0:4 * n].rearrange("p (l c) -> p l c", l=4), AF.Gelu_apprx_tanh)
        if smp is not None:
            Us, H0b = smp
            self.mm(self.ps[2][:, 0:NSMP], lhsT=kw[:, 0, :], rhs=Us, start=True, stop=False)
            for q in range(4):
                for r in range(2):
                    self.mm(self.ps[2][32 * q:32 * q + 32, 0:NSMP], lhsT=wc[:, q, 0, r, :], rhs=H0b[:, r, 4 * i + q, :],
                            start=False, stop=(r == 1), tp=(0, 32 * q))
            self.act(GY_i[:, seq:seq + NSMP], self.ps[2][:, 0:NSMP], AF.Gelu_apprx_tanh)

    def prefix_front(self, pp, U, load=True):
        nts = [(0, PN)]
        if load:
            self.load_xT(self.din["xcat"][pp * PN:(pp + 1) * PN, :], PN)
        self.rmsnorm("g_mix", nts, PN)
        for i in range(NPT):
            wblk = self.wload(self.wview("w_in", DP + 128 * i, 128), KT, 128)
            b = self.bank % 8
            self.bank += 1
            self.mm_acc([b], wblk, 0, self.hT, KT, nts)
            self.cp("act", U[i][:, 0:PN].rearrange("p (l c) -> p c l", l=LCH),
                    self.ps[b][:, 0:PN].rearrange("p (c l) -> p c l", l=LCH))

    def prefix_s5(self, pp, U, HN, bufs):
        SP, QT, T = bufs
        n = PN // LCH
        for k in range(2):
            self.dma("sp", self.WBs[k], self.WB_d[:, k], "wb%d" % k)
        for i in range(NPT):
            wb = self.WBs[i % 2]
            for r in range(2):
                for l in range(LCH):
                    for q in range(4):
                        self.mm(self.ps[4 + q][:, r * NCHM:r * NCHM + n], lhsT=wb[32 * q:32 * q + 32, l, r, :],
                                rhs=U[i][32 * q:32 * q + 32, l * n:(l + 1) * n], start=(l == 0), stop=(l == LCH - 1),
                                tp=(32 * q, 0))
            for q in range(4):
                self.cp("act", SP[:, :, 4 * i + q, :],
                        self.ps[4 + q][:, 0:2 * NCHM].rearrange("p (r c) -> p r c", r=2)[:, :, 0:n])
            if i + 2 < NPT:
                self.dma("sp", self.WBs[i % 2], self.WB_d[:, i + 2], "wb%d" % (i % 2))
        Qr, Qi = QT[:, 0, :, 1:n + 1], QT[:, 1, :, 1:n + 1]
        Sr, Si = SP[:, 0], SP[:, 1]
        self.tt("dve", T[0], Qr, Sr, ALU.mult)
        self.tt("dve", T[1], Qi, Si, ALU.mult)
        self.tt("dve", T[2], Qr, Si, ALU.mult)
        self.tt("dve", T[3], Qi, Sr, ALU.mult)
        self.tt("dve", T[0], T[0], T[1], ALU.subtract)
        self.tt("dve", T[2], T[2], T[3], ALU.add)
        self.red(HN[:, 0, :], T[0])
        self.red(HN[:, 1, :], T[2])
        t1, t2 = self.tmp32
        q0r, q0i = QT[:, 0, :, 0], QT[:, 1, :, 0]
        hr, hi = self.Hc[:, 0, :], self.Hc[:, 1, :]
        self.tt("dve", t1, q0r, hr, ALU.mult)
        self.tt("dve", t2, q0i, hi, ALU.mult)
        self.tt("dve", t1, t1, t2, ALU.subtract)
        self.tt("dve", t2, q0r, hi, ALU.mult)
        self.tt("dve", hi, q0i, hr, ALU.mult)
        self.tt("dve", hi, hi, t2, ALU.add)
        self.tt("dve", hi, hi, HN[:, 1, :], ALU.add)
        self.tt("dve", hr, t1, HN[:, 0, :], ALU.add)

    def emit_chunk(self, ci, R0, R1):
        din = self.din
        seq = SEQS[ci]
        nsmp = NC - seq
        n = seq // LCH
        nts = NT
        col0 = PFX + ci * NC
        reg = Region(R0, R1)
        A = lambda name, shape, dt=F32: self.alloc(reg, name, shape, dt)
        UP = [A("UP%d" % k, [128, 16 + NC]) for k in range(NPT)]
        GY = [A("GY%d" % k, [128, NC], F32R) for k in range(NPT)]
        ypoff = reg.cur
        YP = [A("YP%d" % k, [128, NC], F32R) for k in range(NPT)]
        ypend = reg.cur
        uoff = reg.cur
        U = [A("U%d" % k, [128, NC], BF16) for k in range(NPT)]
        s5off = reg.cur
        self.s5_alloc(reg)
        ry = Region(uoff, R1)
        Y2 = [self.alloc(ry, "Y2_%d" % k, [128, NC], F32R) for k in range(NPT)]
        self.s5off, self.R1 = ry.cur, R1
        DF = [self.alloc(Region(self.allocs[GY[k].name][0], self.allocs[GY[k].name][1]), "DF%d" % k, [128, NC], F32R) for k in range(NPT)]
        rx = Region(ypoff, ypend)
        self.xtok = [self.alloc(rx, "xtok%d" % k, [128, 2048]) for k in range(2)]
        rs = Region(ypoff, ypend)
        self.Ss8 = list(self.Ss) + [self.alloc(rs, "Ss8_%d" % k, [128, 2, 4, NCHM]) for k in range(6)]
        rm = Region(self.s5off, R1)
        MG = [self.alloc(rm, "MG%d" % k, [128, NC], F32R) for k in range(8)]
        if ci == 0 and getattr(self, "chunk0_preloaded", False):
            pass
        else:
            if getattr(self, "XB", None) is not None:
                for k in range(KT):
                    self.cp(("act", "dve")[k % 2], self.xT[k], self.XB[k])
                self.XB = None
            elif not (ci == 0 and getattr(self, "chunk0_loaded", False)):
                self.load_xT(din["xcat"][col0:col0 + NC, :], NC)
            self.rmsnorm("g_mix", nts, NC)
        for k in range(NPT):
            self.cp("dve", UP[k][:, 0:16], self.TAILP[:, k, :])
        for m in range(2 * NPT):
            wblk = self.wload(self.wview("w_in", 128 * m, 128), KT, 128)
            b = self.bank % 8
            self.bank += 2
            self.mm_acc([b, (b + 1) % 8], wblk, 0, self.hT, KT, nts)
            for nn, (a, e) in enumerate(nts):
                pst = self.ps[(b + nn) % 8]
                if m < NPT:
                    self.cp("act", UP[m][:, 16 + a:16 + e], pst[:, 0:e - a])
                else:
                    i = m - NPT
                    se = min(e, seq)
                    self.cp("act", U[i][:, 0:seq].rearrange("p (l c) -> p c l", l=LCH)[:, a // LCH:se // LCH, :],
                            pst[:, 0:se - a].rearrange("p (c l) -> p c l", l=LCH))
                    if e > seq:
                        self.cp("act", U[i][:, seq:NC], pst[:, seq - a:e - a])
        smp = None
        if nsmp:
            smp_state = self.sample_state_in(reg)
        self.cmul("dve", self.INI[:, 0, :], self.INI[:, 1, :], self.Hc[:, 0, :], self.Hc[:, 1, :], self.u8c, self.u8s,
                  self.tmp32b[0], self.tmp32b[1])
        for k in range(2):
            self.s5_fetch_wb(k, k)
            self.s5_fetch_y(k, k)
        for i in range(NPT):
            self.s5_mm(i, U[i], n, i % 2)
            if i + 2 < NPT:
                self.s5_fetch_wb(i + 2, i % 2)
        for i in range(NPT):
            self.s5_chain(i, n, i % 2)
            sm = (U[i][:, seq:NC], self.H0b) if nsmp else None
            self.s5_y(i, U[i], n, i % 2, GY[i], seq, sm)
            if i + 2 < NPT:
                self.s5_fetch_y(i + 2, i % 2)
        if nsmp:
            self.sample_state_out(U, seq)
        for m in range(NPT):
            wa = self.wload(self.wview("w_glu", 128 * m, 128), NPT, 128)
            wb = self.wload(self.wview("w_glu", DP + 128 * m, 128), NPT, 128)
            b = self.bank % 8
            self.bank += 4
            self.mm_acc([b, (b + 1) % 8], wa, 0, GY, NPT, nts)
            self.mm_acc([(b + 2) % 8, (b + 3) % 8], wb, 0, GY, NPT, nts)
            for nn, (a, e) in enumerate(nts):
                sg = self.next_sig()
                self.act(sg[:, 0:e - a], self.ps[(b + 2 + nn) % 8][:, 0:e - a], AF.Sigmoid)
                self.tt("dve", Y2[m][:, a:e], self.ps[(b + nn) % 8][:, 0:e - a], sg[:, 0:e - a], ALU.mult)
        self.pool_mixer(ci, UP, DF, YP, seq, nsmp, reg)
        for mg in range(4):
            for mm_ in range(4):
                m = 4 * mg + mm_
                wgp = self.wload(self.wview("w_in", 2 * DP + 128 * m, 128), KT, 128)
                wgs = self.wload(self.wview("w_in", 2 * DP + DM + 128 * m, 128), KT, 128)
                wbp = self.wload(self.wview("w_branch_pool", 128 * m, 128), NPT, 128)
                wbs = self.wload(self.wview("w_branch_ssm", 128 * m, 128), NPT, 128)
                for nn, (a, e) in enumerate(nts):
                    b = 4 * (nn % 2)
                    w = e - a
                    self.mm_acc([b], wgp, 0, self.hT, KT, [(a, e)])
                    self.mm_acc([b + 1], wgs, 0, self.hT, KT, [(a, e)])
                    self.mm_acc([b + 2], wbp, 0, YP, NPT, [(a, e)])
                    self.mm_acc([b + 3], wbs, 0, Y2, NPT, [(a, e)])
                    s1, s2 = self.next_sig(), self.next_sig()
                    self.act(s1[:, 0:w], self.ps[b][:, 0:w], AF.Sigmoid)
                    self.act(s2[:, 0:w], self.ps[b + 1][:, 0:w], AF.Sigmoid)
                    self.tt("dve", s1[:, 0:w], s1[:, 0:w], self.ps[b + 2][:, 0:w], ALU.mult)
                    self.tt("dve", s2[:, 0:w], s2[:, 0:w], self.ps[b + 3][:, 0:w], ALU.mult)
                    self.tt("dve", MG[4 * (mg % 2) + mm_][:, a:e], s1[:, 0:w], s2[:, 0:w], ALU.add)
            mgt = MG[4 * (mg % 2):4 * (mg % 2) + 4]
            for mo4 in range(4):
                wblk = self.wload(self.wview("w_out", 512 * mo4, 512, k0=4 * mg, kt=4), 4, 512)
                for mo in range(4):
                    b = self.bank % 8
                    self.bank += 2
                    self.mm_acc([b, (b + 1) % 8], wblk, 128 * mo, mgt, 4, nts)
                    for nn, (a, e) in enumerate(nts):
                        xt = self.xT[4 * mo4 + mo][:, a:e]
                        self.tt("dve", xt, xt, self.ps[(b + nn) % 8][:, 0:e - a], ALU.add)
        self.ffn(ci, seq, nsmp, R0, R1)
        self.ple_and_out(ci, seq, nsmp, R0, R1)

    def sample_state_in(self, reg):
        self.H0f = self.alloc(reg, "H0f", [128, 2, 32, NSMP])
        self.H0b = self.alloc(reg, "H0b", [128, 2, 32, NSMP], BF16)
        for r, nm in enumerate(("st_re", "st_im")):
            for hh in range(2):
                st = self.xtok[hh]
                self.dma("act", st[0:NSMP, :], self.din[nm][:, 2048 * hh:2048 * hh + 2048], "xin%d" % hh)
                b = self.bank % 8
                self.bank += 1
                for jj in range(16):
                    self.tr(self.ps[b][:, NSMP * jj:NSMP * jj + NSMP], st[0:NSMP, 128 * jj:128 * jj + 128],
                            self.ident[0:NSMP, 0:NSMP])
                src = self.ps[b][:, 0:16 * NSMP].rearrange("p (j s) -> p j s", j=16)
                self.cp("act", self.H0f[:, r, 16 * hh:16 * hh + 16, :], src)
                self.cp("dve", self.H0b[:, r, 16 * hh:16 * hh + 16, :], src)

    def sample_state_out(self, U, seq):
        rm = Region(self.s5off, self.R1)
        wb7 = self.alloc(rm, "wb7", [128, 8, 2, 128], BF16)
        BU = self.alloc(rm, "BU", [128, 2, 32, NSMP])
        HN = self.alloc(rm, "HN", [128, 2, 32, NSMP])
        T1 = self.alloc(rm, "T1s", [128, 32, NSMP])
        T2 = self.alloc(rm, "T2s", [128, 32, NSMP])
        self.dma("act", wb7, self.WB_d[:, :, 7], "wb7")
        for i in range(NPT):
            for r in range(2):
                for q in range(4):
                    c0 = (i * 2 + r) * NSMP
                    self.mm(self.ps[4 + q][:, c0:c0 + NSMP], lhsT=wb7[32 * q:32 * q + 32, i, r, :],
                            rhs=U[i][32 * q:32 * q + 32, seq:NC], start=True, stop=True, tp=(32 * q, 0))
        for q in range(4):
            dst = BU.rearrange("p r (i q) s -> p q i r s", q=4)[:, q]
            self.cp("act", dst, self.ps[4 + q][:, 0:16 * NSMP].rearrange("p (i r s) -> p i r s", i=8, r=2))
        arb = bc(self.ar1.unsqueeze(2), [128, 32, NSMP])
        aib = bc(self.ai1.unsqueeze(2), [128, 32, NSMP])
        self.cmul("dve", HN[:, 0], HN[:, 1], self.H0f[:, 0], self.H0f[:, 1], arb, aib, T1, T2)
        self.tt("dve", HN, HN, BU, ALU.add)
        for r, nm in enumerate(("o_sre", "o_sim")):
            for hh in range(2):
                st = self.xtok[hh]
                for g4 in range(4):
                    b = self.bank % 8
                    self.bank += 1
                    for jj in range(4):
                        j = 16 * hh + 4 * g4 + jj
                        self.tr(self.ps[b][0:NSMP, 128 * jj:128 * jj + 128], HN[:, r, j, :], self.ident)
                    self.cp("act", st[0:NSMP, 512 * g4:512 * g4 + 512], self.ps[b][0:NSMP, 0:512])
                self.dma("act", self.din[nm][:, 2048 * hh:2048 * hh + 2048], st[0:NSMP, :], "xout%d" % hh)

    def pool_mixer(self, ci, UP, DF, YP, seq, nsmp, reg):
        rm = Region(self.s5off, self.R1)
        TA = [self.alloc(rm, "TA%d" % k, [128, 16 + NC]) for k in range(2)]
        TB = [self.alloc(rm, "TB%d" % k, [128, 16 + NC]) for k in range(2)]
        tq = self.alloc(rm, "tq", [128, 16])
        E = 16 + seq
        if nsmp:
            CPT = [self.alloc(rm, "CPT%d" % k, [128, 240]) for k in range(NPT)]
            cprow = self.din["cpool"].rearrange("s r c -> (s r) c")
            xt = self.xtok[0]
            self.dma("act", xt[:, 0:1024], cprow[0:128, :], "xin0")
            self.dma("act", xt[0:112, 1024:2048], cprow[128:240, :], "xin1")
            for m in range(NPT):
                b = self.bank % 8
                self.bank += 1
                self.tr(self.ps[b][:, 0:128], xt[:, 128 * m:128 * m + 128], self.ident)
                self.tr(self.ps[b][:, 128:240], xt[0:112, 1024 + 128 * m:1024 + 128 * m + 128], self.ident[0:112, 0:112])
                self.cp("act", CPT[m], self.ps[b][:, 0:240])
        wblk = self.wload(self.din["w_pool"].rearrange("g (kk p) c -> p (g kk) c", p=128), 8, 256)
        for k in range(4):
            w = 2 << k
            for mm_ in range(2):
                m = 2 * k + mm_
                x = UP[m]
                cur = x
                sh = 1
                idx = 0
                while sh < w:
                    dst = (TA if idx % 2 == 0 else TB)[mm_]
                    self.tt("dve", dst[:, sh:E], cur[:, sh:E], cur[:, 0:E - sh], ALU.add)
                    self.cp("dve", dst[:, 0:sh], cur[:, 0:sh])
                    cur = dst
                    sh *= 2
                    idx += 1
                self.stt(DF[m][:, 0:seq], cur[:, 16:E], 1.0 / w, x[:, 16:E], ALU.mult, ALU.subtract)
                if ci == 0:
                    self.tt("dve", tq, cur[:, 16 + PRE:32 + PRE], self.invc[:, k, :], ALU.mult)
                    self.tt("dve", DF[m][:, PRE:PRE + 16], tq, x[:, 16 + PRE:32 + PRE], ALU.subtract)
                if nsmp:
                    us = x[:, E:E + NSMP]
                    self.red(tq, CPT[m].rearrange("p (s r) -> p s r", r=15)[:, :, 16 - w:15])
                    self.tt("dve", tq, tq, us, ALU.add)
                    self.stt(DF[m][:, seq:NC], tq, 1.0 / w, us, ALU.mult, ALU.subtract)
                self.cp("dve", self.TAILP[:, m, :], x[:, E - 16:E])
            for mo2 in range(2):
                b = self.bank % 8
                self.bank += 2
                self.mm_acc([b, (b + 1) % 8], wblk[:, 2 * k:2 * k + 2, :], 128 * mo2, [DF[2 * k], DF[2 * k + 1]], 2, NT)
                for nn, (a, e) in enumerate(NT):
                    self.act(YP[2 * k + mo2][:, a:e], self.ps[(b + nn) % 8][:, 0:e - a], AF.Copy,
                             scale=self.pscale[:, 2 * k + mo2:2 * k + mo2 + 1])
        if nsmp:
            stg = self.alloc(rm, "stgp", [16, 2048])
            for (c0, nr, dst) in ((E - 15, 15, self.din["o_poolp"]), (E, NSMP, self.din["o_pools"][:, 14, :])):
                for hb in range(2):
                    b = self.bank % 8
                    self.bank += 1
                    for mm_ in range(4):
                        m = 4 * hb + mm_
                        self.tr(self.ps[b][0:nr, 128 * mm_:128 * mm_ + 128], UP[m][:, c0:c0 + nr], self.ident)
                    self.cp("act", stg[0:nr, (0 if nr == 15 else 1024) + 512 * hb:(0 if nr == 15 else 1024) + 512 * hb + 512],
                            self.ps[b][0:nr, 0:512])
                o0 = 0 if nr == 15 else 1024
                self.dma("act", dst, stg[0:nr, o0:o0 + 1024], "xout%d" % (0 if nr == 15 else 1))
            self.dma("act", self.din["o_pools"][:, 0:14, :], self.din["cpool"][:, 1:15, :], "d2d0")

    def ffn(self, ci, seq, nsmp, R0, R1):
        reg = Region(R0, R1)
        A = lambda name, shape, dt=F32: self.alloc(reg, name, shape, dt)
        AS = [A("AS%d" % k, [128, 2 + NC]) for k in range(2)]
        CV = [A("CV%d" % k, [128, NC]) for k in range(2)]
        GE = [A("GE%d" % k, [128, NC]) for k in range(2)]
        ACTB = [[A("ACTB%d_%d" % (g, k), [128, NC], F32R) for k in range(6)] for g in range(2)]
        CO = A("CO", [32, DFF])
        wc = self.wconv
        XB = None
        if ci == 0:
            rtop = Region((R1 - KT * NC * 4 - 64) // 32 * 32, R1)
            XB = [self.alloc(rtop, "XB%d" % k, [128, NC]) for k in range(KT)]
            rco = Region(*self.allocs[CO.name])
            xstg = [self.alloc(rco, "xstg%d" % k, [128, 2048]) for k in range(2)]
        if nsmp:
            CCT = A("CCT", [128, NFT, 32])
            ASMP = A("ASMP", [128, NFT, NSMP])
            self.dma("act", CO, self.din["cconv"].rearrange("s r c -> (s r) c"), "co")
            for g in range(3):
                b = self.bank % 8
                self.bank += 1
                for jj in range(16):
                    j = 16 * g + jj
                    self.tr(self.ps[b][:, 32 * jj:32 * jj + 32], CO[0:32, 128 * j:128 * j + 128], self.ident[0:32, 0:32])
                self.cp("act", CCT[:, 16 * g:16 * g + 16, :], self.ps[b][:, 0:512].rearrange("p (j s) -> p j s", j=16))
        self.rmsnorm("g_ffn", NT, NC)
        for ga in range(8):
            acts = ACTB[ga % 2]
            if XB is not None and ga == 2:
                c1 = PFX + (ci + 1) * NC
                self.load_xT(self.din["xcat"][c1:c1 + NC, :], NC, dst=XB, stg=xstg)
                self.XB = XB
            for jj in range(6):
                j = 6 * ga + jj
                wa = self.wload(self.wview("w_up", 128 * j, 128), KT, 128)
                wv = self.wload(self.wview("w_up", DFF + 128 * j, 128), KT, 128)
                b = 4 * (j % 2)
                self.mm_acc([b, b + 1], wa, 0, self.hT, KT, NT)
                self.mm_acc([b + 2, b + 3], wv, 0, self.hT, KT, NT)
                a_s, cv, ge = AS[j % 2], CV[j % 2], GE[j % 2]
                self.cp("act", a_s[:, 0:2], self.CT[:, j, :])
                for nn, (a, e) in enumerate(NT):
                    self.cp("act", a_s[:, 2 + a:2 + e], self.ps[b + nn][:, 0:e - a])
                self.act(cv, a_s[:, 2:2 + NC], AF.Identity, scale=wc[:, 2, j:j + 1], bias=self.bconv[:, j:j + 1])
                self.stt(cv, a_s[:, 1:1 + NC], wc[:, 1, j:j + 1], cv, ALU.mult, ALU.add)
                self.stt(cv, a_s[:, 0:NC], wc[:, 0, j:j + 1], cv, ALU.mult, ALU.add)
                if nsmp:
                    cc = CCT[:, j, :].rearrange("p (s r) -> p s r", r=2)
                    cs_ = cv[:, seq:NC]
                    self.act(cs_, a_s[:, 2 + seq:2 + NC], AF.Identity, scale=wc[:, 2, j:j + 1], bias=self.bconv[:, j:j + 1])
                    self.stt(cs_, cc[:, :, 1], wc[:, 1, j:j + 1], cs_, ALU.mult, ALU.add)
                    self.stt(cs_, cc[:, :, 0], wc[:, 0, j:j + 1], cs_, ALU.mult, ALU.add)
                    self.cp("act", ASMP[:, j, :], a_s[:, 2 + seq:2 + NC])
                self.cp("act", self.CT[:, j, :], a_s[:, seq:seq + 2])
                self.act(ge, cv, AF.Gelu_apprx_tanh)
                for nn, (a, e) in enumerate(NT):
                    self.tt("dve", acts[jj][:, a:e], ge[:, a:e], self.ps[b + 2 + nn][:, 0:e - a], ALU.mult)
            for mo2 in range(8):
                wblk = self.wload(self.wview("w_down", 256 * mo2, 256, k0=6 * ga, kt=6), 6, 256)
                for mo in range(2):
                    b = self.bank % 8
                    self.bank += 2
                    self.mm_acc([b, (b + 1) % 8], wblk, 128 * mo, acts, 6, NT)
                    for nn, (a, e) in enumerate(NT):
                        xt = self.xT[2 * mo2 + mo][:, a:e]
                        self.tt("dve", xt, xt, self.ps[(b + nn) % 8][:, 0:e - a], ALU.add)
        if nsmp:
            for (src_fn, nr, dst, key) in ((lambda j: self.CT[:, j, :], 2, self.din["o_convp"], "co"),
                                           (lambda j: ASMP[:, j, :], NSMP, self.din["o_convs"][:, 1, :], "co")):
                for g in range(12):
                    b = self.bank % 8
                    self.bank += 1
                    for jj in range(4):
                        j = 4 * g + jj
                        self.tr(self.ps[b][0:nr, 128 * jj:128 * jj + 128], src_fn(j), self.ident)
                    self.cp("act", CO[0:nr, 512 * g:512 * g + 512], self.ps[b][0:nr, 0:512])
                self.dma("act", dst, CO[0:nr, :], key)
            self.dma("act", self.din["o_convs"][:, 0, :], self.din["cconv"][:, 1, :], "d2d1")

    def ple_and_out(self, ci, seq, nsmp, R0, R1):
        reg = Region(R0, R1)
        A = lambda name, shape, dt=F32: self.alloc(reg, name, shape, dt)
        pT = [A("pT%d" % k, [128, NC], F32R) for k in range(2)]
        ptok = [A("ptok%d" % k, [128, DPLE]) for k in range(2)]
        OST = [A("OST%d" % k, [128, 2048]) for k in range(2)]
        prow = self.din["pcat"][ci * NC:(ci + 1) * NC, :]
        t0, it = 0, 0
        while t0 < NC:
            nt = min(128, NC - t0)
            pt = ptok[it % 2]
            self.dma("act", pt[0:nt, :], prow[t0:t0 + nt, :], "pin%d" % (it % 2))
            b = self.bank % 8
            self.bank += 1
            for kk in range(2):
                self.tr(self.ps[b][:, 128 * kk:128 * kk + nt], pt[0:nt, 128 * kk:128 * kk + 128], self.ident[0:nt, 0:nt])
                self.cp("act", pT[kk][:, t0:t0 + nt], self.ps[b][:, 128 * kk:128 * kk + nt])
            t0 += nt
            it += 1
        self.rmsnorm("g_ple", NT, NC)
        for mo in range(KT):
            wg = self.wload(self.wview("w_ple_gate", 128 * mo, 128), KT, 128)
            wp = self.wload(self.wview("w_ple", 128 * mo, 128), 2, 128)
            b = 4 * (mo % 2)
            self.mm_acc([b, b + 1], wg, 0, self.hT, KT, NT)
            self.mm_acc([b + 2, b + 3], wp, 0, pT, 2, NT)
            for nn, (a, e) in enumerate(NT):
                sg = self.next_sig()
                w = e - a
                self.act(sg[:, 0:w], self.ps[b + nn][:, 0:w], AF.Sigmoid)
                self.tt("dve", sg[:, 0:w], sg[:, 0:w], self.ps[b + 2 + nn][:, 0:w], ALU.mult)
                xt = self.xT[mo][:, a:e]
                self.tt("dve", xt, xt, sg[:, 0:w], ALU.add)
        yT = [A("yT%d" % k, [128, NC]) for k in range(KT)]
        self.rmsnorm("g_final", NT, NC, out=yT)
        c_lo = PRE if ci == 0 else 0
        row0 = 0 if ci == 0 else SEQS[0] - PRE
        t0, it = c_lo, 0
        while t0 < NC:
            nt = min(128, NC - t0)
            ost = OST[it % 2]
            for kg in range(4):
                b = self.bank % 8
                self.bank += 1
                for kk in range(4):
                    k = 4 * kg + kk
                    self.tr(self.ps[b][0:nt, 128 * kk:128 * kk + 128], yT[k][:, t0:t0 + nt], self.ident)
                self.cp("act" if kg % 2 == 0 else "dve", ost[0:nt, 512 * kg:512 * kg + 512], self.ps[b][0:nt, 0:512])
            r0 = row0 + (t0 - c_lo)
            self.dma("act", self.din["o_y"][r0:r0 + nt, :], ost[0:nt, :], "yout%d" % (it % 2))
            t0 += nt
            it += 1


W_SPECS = (("w_in", [DM, DIN]), ("w_pool", [4, 256, 256]), ("w_glu", [DP, 2 * DP]), ("w_branch_pool", [DP, DM]),
           ("w_branch_ssm", [DP, DM]), ("w_out", [DM, DM]), ("w_up", [DM, 2 * DFF]), ("w_down", [DFF, DM]),
           ("w_ple_gate", [DM, DM]), ("w_ple", [DPLE, DM]), ("ones", [128, 128]))
V_SPECS = (("xcat", [PFX + 2 * NC, DM]), ("pcat", [2 * NC, DPLE]), ("cpool", [NSMP, 15, DP]), ("st_re", [NSMP, 4096]),
           ("st_im", [NSMP, 4096]), ("cconv", [NSMP, 2, DFF]), ("invc", [128, 4, 16]), ("ident", [128, 128]),
           ("mask", [128, 128]), ("vecs", [128, 272]), ("lamv", [128, 3, 32]), ("b_re", [64, 64, 16]),
           ("b_im", [64, 64, 16]), ("c_re", [64, 16, 64]), ("c_im", [64, 16, 64]))
O_SPECS = (("o_y", [OWN + NSMP, DM]), ("o_poolp", [15, DP]), ("o_hp", [2, 128, 32]), ("o_convp", [2, DFF]),
           ("o_pools", [NSMP, 15, DP]), ("o_sre", [NSMP, 4096]), ("o_sim", [NSMP, 4096]), ("o_convs", [NSMP, 2, DFF]))


def build_program(stop_after=None, dbg=False):
    kb = Step()
    nc = kb.nc
    din = {}
    for nm, shp in W_SPECS:
        din[nm] = nc.dram_tensor(nm, shp, F32R, kind="ExternalInput").ap()
    for nm, shp in V_SPECS:
        din[nm] = nc.dram_tensor(nm, shp, F32, kind="ExternalInput").ap()
    for nm, shp in O_SPECS:
        din[nm] = nc.dram_tensor(nm, shp, F32, kind="ExternalOutput").ap()
    TOP = 229120
    pers = Region(256, TOP)
    kb.P.bulk_keys.add("su")
    kb.P.bulk_keys.add("su0")
    kb.init_main(din, pers)
    setup_pers = Region(pers.take(5376), TOP)
    setup_pers.hi = setup_pers.lo + 5376
    R0 = (setup_pers.hi + 31) // 32 * 32
    kb.R0, kb.R1 = R0, TOP
    PBUF = 2 * 8 * 992 + 8192 + 2 * 256 + 2 * 128 + 256
    ptop = Region((TOP - PBUF - 256) // 32 * 32, TOP)
    Upf = [[kb.alloc(ptop, "Upf%d_%d" % (pp, k), [128, PN], BF16) for k in range(NPT)] for pp in range(2)]
    kb.xtok = [kb.alloc(ptop, "xtokp0", [128, 2048])] * 2
    HN = kb.alloc(ptop, "HNp", [128, 2, 32])
    kb.tmp32 = [kb.alloc(ptop, "tmp32_%d" % k, [128, 32]) for k in range(2)]
    kb.stop = stop_after
    kb.xcopy_act = True
    gen = kb.emit_setup(din, setup_pers, Region(R0, ptop.lo), dbg=dbg)
    next(gen)
    kb.prefix_front(0, Upf[0])
    next(gen)
    kb.load_xT(din["xcat"][PN:2 * PN, :], PN)
    next(gen)
    kb.prefix_front(1, Upf[1], load=False)
    kb.load_xT(din["xcat"][PFX:PFX + NC, :], NC)
    kb.chunk0_loaded = True
    for _ in gen:
        pass
    kb.xcopy_act = False
    if stop_after == "setup":
        kb.P.emit()
        return kb
    reg = Region(R0, ptop.lo)
    kb.WBs = [kb.alloc(reg, "WBp%d" % k, [128, 8, 2, 128], BF16) for k in range(2)]
    n = PN // LCH
    SP = kb.alloc(reg, "SPp", [128, 2, 32, n])
    QT = kb.alloc(reg, "QTp", [128, 2, 32, n + 1])
    T = [kb.alloc(reg, "Tp%d" % k, [128, 32, n]) for k in range(4)]
    kb.dma("sp", QT, kb.PQ_d.rearrange("t p j c -> p t j c"), "qtp")
    kb.prefix_s5(0, Upf[0], HN, (SP, QT, T))
    kb.prefix_s5(1, Upf[1], HN, (SP, QT, T))
    if stop_after == "prefix":
        kb.dma("sp", din["o_hp"][0], kb.Hc[:, 0, :], "ohp0")
        kb.dma("sp", din["o_hp"][1], kb.Hc[:, 1, :], "ohp1")
        kb.P.emit()
        return kb
    for ci in range(2):
        kb.emit_chunk(ci, R0, TOP)
    kb.dma("sp", din["o_hp"][0], kb.Hc[:, 0, :], "ohp0")
    kb.dma("sp", din["o_hp"][1], kb.Hc[:, 1, :], "ohp1")
    kb.P.emit()
    return kb


POOL_W = (2, 4, 8, 16)


def make_in_maps(inp):
    f = lambda a: np.ascontiguousarray(a, dtype=np.float32)
    shared = {
        "w_in": f(inp["w_in"][0]), "w_pool": f(inp["w_pool"][0]), "w_glu": f(inp["w_glu"][0]),
        "w_branch_pool": f(inp["w_branch_pool"][0]), "w_branch_ssm": f(inp["w_branch_ssm"][0]),
        "w_out": f(inp["w_out"][0]), "w_up": f(inp["w_up"][0]), "w_down": f(inp["w_down"][0]),
        "w_ple_gate": f(inp["w_ple_gate"][0]), "w_ple": f(inp["w_ple"][0]),
        "ones": np.ones((128, 128), np.float32), "ident": np.eye(128, dtype=np.float32),
        "mask": np.kron(np.eye(8, dtype=np.float32), np.ones((16, 16), np.float32)),
        "b_re": f(inp["ssm_b_re"][0]), "b_im": f(inp["ssm_b_im"][0]),
        "c_re": f(inp["ssm_c_re"][0]), "c_im": f(inp["ssm_c_im"][0]),
    }
    col = lambda v: np.asarray(v, np.float32).reshape(-1, 128).T
    shared["vecs"] = f(np.concatenate([col(inp["g_mix"][0]), col(inp["g_ffn"][0]), col(inp["g_ple"][0]), col(inp["g_final"]),
                                       col(inp["pool_scale"][0]), col(inp["ssm_d"][0]), col(inp["w_conv"][0][0]),
                                       col(inp["w_conv"][0][1]), col(inp["w_conv"][0][2]), col(inp["b_conv"][0])], axis=1))
    sm = lambda a: np.asarray(a, np.float32).reshape(32, 128).T
    ldt = np.repeat(np.asarray(inp["ssm_log_dt"][0], np.float32)[:, None], 64, axis=1)
    shared["lamv"] = f(np.stack([sm(inp["ssm_lam_re"][0]), sm(inp["ssm_lam_im"][0]), sm(ldt)], axis=1))
    xp, xs = inp["x_prompt"], inp["x_sample"][:, 0, :]
    pp, psm = inp["p_prompt"][0], inp["p_sample"][0][:, 0, :]
    maps = []
    for c in range(8):
        b, half = c // 2, c % 2
        m = dict(shared)
        s0 = half * OWN
        first = xp[b, 0:OWN] if half else np.zeros((OWN, DM), np.float32)
        m["xcat"] = f(np.concatenate([first, xp[b, s0:s0 + OWN], xs[NSMP * c:NSMP * c + NSMP]], axis=0))
        pfirst = pp[b, s0 - PRE:s0] if half else np.zeros((PRE, DPLE), np.float32)
        m["pcat"] = f(np.concatenate([pfirst, pp[b, s0:s0 + OWN], psm[NSMP * c:NSMP * c + NSMP]], axis=0))
        m["cpool"] = f(inp["cache_pool"][0, NSMP * c:NSMP * c + NSMP])
        m["st_re"] = f(inp["state_ssm_re"][0, NSMP * c:NSMP * c + NSMP].reshape(NSMP, 4096))
        m["st_im"] = f(inp["state_ssm_im"][0, NSMP * c:NSMP * c + NSMP].reshape(NSMP, 4096))
        m["cconv"] = f(inp["cache_conv"][0, NSMP * c:NSMP * c + NSMP])
        iv = np.zeros((128, 4, 16), np.float32)
        for k, w in enumerate(POOL_W):
            for t in range(16):
                iv[:, k, t] = 1.0 / (w if half else min(w, t + 1))
        m["invc"] = iv
        maps.append(m)
    return maps


_PROG = {}


def kernel(**inputs):
    inp = {k: np.asarray(v) for k, v in inputs.items()}
    if "kb" not in _PROG:
        _PROG["kb"] = build_program()
    kb = _PROG["kb"]
    maps = make_in_maps(inp)
    res = run_bass_kernel_spmd(kb.nc, maps, core_ids=list(range(8)))
    R = res.results
    B = 4
    y_p = np.zeros((B, 2 * OWN, DM), np.float32)
    y_s = np.zeros((128, 1, DM), np.float32)
    pool_p = np.zeros((1, B, 15, DP), np.float32)
    re_p = np.zeros((1, B, 64, 64), np.float32)
    im_p = np.zeros((1, B, 64, 64), np.float32)
    conv_p = np.zeros((1, B, 2, DFF), np.float32)
    pool_s = np.zeros((1, 128, 15, DP), np.float32)
    re_s = np.zeros((1, 128, 64, 64), np.float32)
    im_s = np.zeros((1, 128, 64, 64), np.float32)
    conv_s = np.zeros((1, 128, 2, DFF), np.float32)
    for c in range(8):
        b, half = c // 2, c % 2
        r = R[c]
        y_p[b, half * OWN:(half + 1) * OWN] = r["o_y"][0:OWN]
        y_s[NSMP * c:NSMP * c + NSMP, 0] = r["o_y"][OWN:OWN + NSMP]
        if half:
            pool_p[0, b] = r["o_poolp"]
            re_p[0, b] = r["o_hp"][0].T.reshape(64, 64)
            im_p[0, b] = r["o_hp"][1].T.reshape(64, 64)
            conv_p[0, b] = r["o_convp"]
        pool_s[0, NSMP * c:NSMP * c + NSMP] = r["o_pools"]
        re_s[0, NSMP * c:NSMP * c + NSMP] = r["o_sre"].reshape(NSMP, 64, 64)
        im_s[0, NSMP * c:NSMP * c + NSMP] = r["o_sim"].reshape(NSMP, 64, 64)
        conv_s[0, NSMP * c:NSMP * c + NSMP] = r["o_convs"]
    return (y_p, y_s, pool_p, re_p, im_p, conv_p, pool_s, re_s, im_s, conv_s)
```
